# Optimizing a Trainium2 kernel written in Bass

```python
import jax, jax.numpy as jnp
from jax import lax
import numpy as np

D_MODEL = 1024
BATCH = 16
SEQ = 2048
DEPTH = 2
DEC_BATCH = 128
DEC_SEQ = 8
PAST_LEN = 16384
PAGE_SIZE = 128

MIX_W = D_MODEL
ATTN_W = MIX_W // 2
HEAD_DIM = 64
N_HEADS = ATTN_W // HEAD_DIM
N_KV_HEADS = 2
GROUP = N_HEADS // N_KV_HEADS
KV_W = N_KV_HEADS * HEAD_DIM
LRU_W = MIX_W - ATTN_W
N_LRU_BLOCKS = 8
LRU_BLOCK = LRU_W // N_LRU_BLOCKS
CONV_WIDTH = 4
RG_C = 8.0
WINDOW = 128
Q_BLOCK = 128
D_FF = ((8 * D_MODEL // 3 + 127) // 128) * 128
N_MOD = 9
FFN_RES = 0.5
IN_COLS = ATTN_W + 2 * KV_W + 2 * LRU_W
SPLITS = (ATTN_W, ATTN_W + KV_W, ATTN_W + 2 * KV_W, ATTN_W + 2 * KV_W + LRU_W)
RMS_EPS = 1e-6
NEG_INF = -1e30

kernel_name = 'hymba_rglru_swa_sink_alibi_macaron_adaln_step'


def rms_norm(x, eps=RMS_EPS):
    xf = x.astype(jnp.float32)
    return (xf * lax.rsqrt(jnp.mean(xf * xf, axis=-1, keepdims=True) + eps)).astype(x.dtype)


def modulate(x, shift, scale):
    return rms_norm(x) * (1 + scale) + shift


def swiglu(h, w_gate, w_up, w_down):
    return (jax.nn.silu(h @ w_gate) * (h @ w_up)) @ w_down


def alibi_slopes():
    return jnp.asarray([2.0 ** (-8.0 * (hh + 1) / N_HEADS) for hh in range(N_HEADS)], dtype=jnp.float32)


def band_attention(q, k, v, dist, valid, sinks):
    b, nb, tq = q.shape[:3]
    qg = q.reshape(b, nb, tq, N_KV_HEADS, GROUP, HEAD_DIM)
    s = jnp.einsum('bnqkgd,bnskd->bnkgqs', qg, k, preferred_element_type=jnp.float32) * (HEAD_DIM ** -0.5)
    slopes = alibi_slopes().reshape(N_KV_HEADS, GROUP, 1, 1)
    s = s - slopes * dist.astype(jnp.float32)
    s = jnp.where(valid[:, None, None], s, NEG_INF)
    sink = jnp.broadcast_to(sinks.astype(jnp.float32).reshape(N_KV_HEADS, GROUP, 1, 1), s.shape[:-1] + (1,))
    p = jax.nn.softmax(jnp.concatenate([s, sink], axis=-1), axis=-1)[..., :-1]
    o = jnp.einsum('bnkgqs,bnskd->bnqkgd', p.astype(v.dtype), v)
    return o.reshape(b, nb * tq, N_HEADS * HEAD_DIM)


def prompt_window_attention(q, k, v, sinks):
    b, t = q.shape[:2]
    nb = t // Q_BLOCK
    qb = q.reshape(b, nb, Q_BLOCK, N_HEADS, HEAD_DIM)

    def with_prev(z):
        zb = z.reshape(b, nb, Q_BLOCK, N_KV_HEADS, HEAD_DIM)
        prev = jnp.concatenate([jnp.zeros_like(zb[:, :1]), zb[:, :-1]], axis=1)
        return jnp.concatenate([prev, zb], axis=2)

    i = jnp.arange(Q_BLOCK)[:, None]
    j = jnp.arange(2 * Q_BLOCK)[None, :]
    dist = Q_BLOCK + i - j
    band = (dist >= 0) & (dist <= WINDOW)
    no_prev = (jnp.arange(nb)[:, None, None] == 0) & (j[None] < Q_BLOCK)
    valid = band[None] & ~no_prev
    o = band_attention(qb, with_prev(k), with_prev(v), dist, valid, sinks)
    return o, k[:, t - WINDOW:], v[:, t - WINDOW:]


def sample_window_attention(q, k, v, k_buf, v_buf, sinks):
    tq = q.shape[1]
    n_buf = k_buf.shape[1]
    kk = jnp.concatenate([k_buf, k], axis=1)
    vv = jnp.concatenate([v_buf, v], axis=1)
    i = jnp.arange(tq)[:, None]
    j = jnp.arange(n_buf + tq)[None, :]
    dist = n_buf + i - j
    valid = ((dist >= 0) & (dist <= WINDOW))[None]
    o = band_attention(q[:, None], kk[:, None], vv[:, None], dist, valid, sinks)
    return o, kk[:, tq:], vv[:, tq:]


def causal_conv(x, buf, w, b):
    t = x.shape[1]
    xp = jnp.concatenate([buf, x], axis=1)
    y = b + xp[:, 0:t] * w[0]
    for tap in range(1, CONV_WIDTH):
        y = y + xp[:, tap:tap + t] * w[tap]
    return y, xp[:, t:]


def rg_lru(xc, h0, w_r, b_r, w_i, b_i, lam):
    b, t, w = xc.shape
    xb = xc.reshape(b, t, N_LRU_BLOCKS, LRU_BLOCK)
    r = jax.nn.sigmoid(jnp.einsum('btnc,ncd->btnd', xb, w_r).reshape(b, t, w) + b_r)
    gi = jax.nn.sigmoid(jnp.einsum('btnc,ncd->btnd', xb, w_i).reshape(b, t, w) + b_i)
    log_a = -RG_C * r.astype(jnp.float32) * jax.nn.softplus(-lam.astype(jnp.float32))
    a = jnp.exp(log_a)
    u = jnp.sqrt(-jnp.expm1(2.0 * log_a)) * (gi * xc).astype(jnp.float32)

    def step(h, au):
        a_t, u_t = au
        h = a_t * h + u_t
        return h, h

    h_last, hs = lax.scan(step, h0.astype(jnp.float32), (jnp.swapaxes(a, 0, 1), jnp.swapaxes(u, 0, 1)))
    return jnp.swapaxes(hs, 0, 1).astype(xc.dtype), h_last.astype(xc.dtype)


def layer(x, c, conv_buf, h0, k_buf, v_buf,
          w_ada, b_ada, w1_gate, w1_up, w1_down, w_in, q_gain, k_gain, sinks,
          conv_w, conv_b, w_rg, b_rg, w_ig, b_ig, lru_lambda, beta_attn, beta_lru,
          w_out, w2_gate, w2_up, w2_down):
    b, t, _ = x.shape
    mod = (jax.nn.silu(c) @ w_ada + b_ada)[:, None, :]
    sh1, sc1, g1, sh2, sc2, g2, sh3, sc3, g3 = jnp.split(mod, N_MOD, axis=-1)

    x = x + FFN_RES * g1 * swiglu(modulate(x, sh1, sc1), w1_gate, w1_up, w1_down)

    h = modulate(x, sh2, sc2)
    z = h @ w_in
    q, k, v, xr, gr = jnp.split(z, SPLITS, axis=-1)
    q = rms_norm(q.reshape(b, t, N_HEADS, HEAD_DIM)) * q_gain
    k = rms_norm(k.reshape(b, t, N_KV_HEADS, HEAD_DIM)) * k_gain
    v = v.reshape(b, t, N_KV_HEADS, HEAD_DIM)
    if k_buf is None:
        attn, new_k, new_v = prompt_window_attention(q, k, v, sinks)
        conv_buf = jnp.zeros((b, CONV_WIDTH - 1, LRU_W), x.dtype)
        h0 = jnp.zeros((b, LRU_W), x.dtype)
    else:
        attn, new_k, new_v = sample_window_attention(q, k, v, k_buf, v_buf, sinks)
    xc, new_conv = causal_conv(xr, conv_buf, conv_w, conv_b)
    hs, h_last = rg_lru(xc, h0, w_rg, b_rg, w_ig, b_ig, lru_lambda)
    lru = hs * jax.nn.gelu(gr)
    merged = jnp.concatenate([rms_norm(attn) * beta_attn, rms_norm(lru) * beta_lru], axis=-1)
    x = x + g2 * (merged @ w_out)

    x = x + FFN_RES * g3 * swiglu(modulate(x, sh3, sc3), w2_gate, w2_up, w2_down)
    return x, new_k, new_v, new_conv, h_last


def setup_inputs(seed: int = 0) -> dict:
    key = jax.random.key(seed)
    ks = list(jax.random.split(key, 40))

    def nrm(shape, scale):
        return jax.random.normal(ks.pop(), shape, jnp.float32) * scale

    u = jax.random.uniform(ks.pop(), (DEPTH, LRU_W), jnp.float32, minval=0.9, maxval=0.999)
    a_base = u ** (1.0 / RG_C)
    lru_lambda = jnp.log(a_base) - jnp.log1p(-a_base)
    return {
        'x_prompt': nrm((BATCH, SEQ, D_MODEL), 1.0),
        'x_sample': nrm((DEC_BATCH, DEC_SEQ, D_MODEL), 1.0),
        'cache_k': nrm((DEPTH, DEC_BATCH, WINDOW, N_KV_HEADS, HEAD_DIM), 1.0),
        'cache_v': nrm((DEPTH, DEC_BATCH, WINDOW, N_KV_HEADS, HEAD_DIM), 1.0),
        'state_conv': nrm((DEPTH, DEC_BATCH, CONV_WIDTH - 1, LRU_W), 1.0),
        'state_lru': nrm((DEPTH, DEC_BATCH, LRU_W), 0.5),
        'c_prompt': nrm((BATCH, D_MODEL), 1.0),
        'c_sample': nrm((DEC_BATCH, D_MODEL), 1.0),
        'w_ada': nrm((DEPTH, D_MODEL, N_MOD * D_MODEL), 0.02),
        'b_ada': nrm((DEPTH, N_MOD * D_MODEL), 0.02),
        'w1_gate': nrm((DEPTH, D_MODEL, D_FF), D_MODEL ** -0.5),
        'w1_up': nrm((DEPTH, D_MODEL, D_FF), D_MODEL ** -0.5),
        'w1_down': nrm((DEPTH, D_FF, D_MODEL), D_FF ** -0.5),
        'w_in': nrm((DEPTH, D_MODEL, IN_COLS), D_MODEL ** -0.5),
        'q_gain': 1.0 + nrm((DEPTH, HEAD_DIM), 0.05),
        'k_gain': 1.0 + nrm((DEPTH, HEAD_DIM), 0.05),
        'sinks': nrm((DEPTH, N_HEADS), 0.5),
        'conv_w': nrm((DEPTH, CONV_WIDTH, LRU_W), CONV_WIDTH ** -0.5),
        'conv_b': nrm((DEPTH, LRU_W), 0.02),
        'w_rg': nrm((DEPTH, N_LRU_BLOCKS, LRU_BLOCK, LRU_BLOCK), LRU_BLOCK ** -0.5),
        'b_rg': nrm((DEPTH, LRU_W), 0.02),
        'w_ig': nrm((DEPTH, N_LRU_BLOCKS, LRU_BLOCK, LRU_BLOCK), LRU_BLOCK ** -0.5),
        'b_ig': nrm((DEPTH, LRU_W), 0.02),
        'lru_lambda': lru_lambda,
        'beta_attn': 1.0 + nrm((DEPTH, ATTN_W), 0.05),
        'beta_lru': 1.0 + nrm((DEPTH, LRU_W), 0.05),
        'w_out': nrm((DEPTH, MIX_W, D_MODEL), MIX_W ** -0.5),
        'w2_gate': nrm((DEPTH, D_MODEL, D_FF), D_MODEL ** -0.5),
        'w2_up': nrm((DEPTH, D_MODEL, D_FF), D_MODEL ** -0.5),
        'w2_down': nrm((DEPTH, D_FF, D_MODEL), D_FF ** -0.5),
    }


def reference(x_prompt, x_sample, cache_k, cache_v, state_conv, state_lru, c_prompt, c_sample,
              w_ada, b_ada, w1_gate, w1_up, w1_down, w_in, q_gain, k_gain, sinks,
              conv_w, conv_b, w_rg, b_rg, w_ig, b_ig, lru_lambda, beta_attn, beta_lru,
              w_out, w2_gate, w2_up, w2_down):
    yp = x_prompt
    ys = x_sample
    kp_l, vp_l, cp_l, hp_l = [], [], [], []
    ks_l, vs_l, cs_l, hs_l = [], [], [], []
    for l in range(DEPTH):
        lw = (w_ada[l], b_ada[l], w1_gate[l], w1_up[l], w1_down[l], w_in[l], q_gain[l], k_gain[l],
              sinks[l], conv_w[l], conv_b[l], w_rg[l], b_rg[l], w_ig[l], b_ig[l], lru_lambda[l],
              beta_attn[l], beta_lru[l], w_out[l], w2_gate[l], w2_up[l], w2_down[l])
        yp, kp, vp, cp, hp = layer(yp, c_prompt, None, None, None, None, *lw)
        ys, kS, vS, cS, hS = layer(ys, c_sample, state_conv[l], state_lru[l], cache_k[l], cache_v[l], *lw)
        kp_l.append(kp); vp_l.append(vp); cp_l.append(cp); hp_l.append(hp)
        ks_l.append(kS); vs_l.append(vS); cs_l.append(cS); hs_l.append(hS)
    return (yp, ys,
            jnp.stack(kp_l), jnp.stack(vp_l), jnp.stack(cp_l), jnp.stack(hp_l),
            jnp.stack(ks_l), jnp.stack(vs_l), jnp.stack(cs_l), jnp.stack(hs_l))
```

```python
import numpy as np
import os
MIXSTOP = float(os.environ.get('MIXSTOP', '99'))
ATTSTOP = int(os.environ.get('ATTSTOP', '99'))
PIPE = int(os.environ.get('PIPE', '1'))
NPF = int(os.environ.get('NPF', '4'))
MSC = float(os.environ.get('MSC', '1.0'))
ENGCONV = int(os.environ.get('ENGCONV', '1'))
CONVC = float(os.environ.get('CONVC', '3.0'))
MQ = os.environ.get('MQ', 'sp')
from contextlib import ExitStack
import concourse.bass as bass
import concourse.mybir as mybir
from concourse.bass_utils import run_bass_kernel_spmd

F32 = mybir.dt.float32
BF16 = mybir.dt.bfloat16
AF = mybir.ActivationFunctionType
ALU = mybir.AluOpType

NCORES = 8
D = 1024
NCH = 8
DFF = 2816
NF = 22
DEPTH = 2
SEQ = 2048
PB = 2
SB = 16
SL = 8
NROW = PB + SB
WIN_COLS = 1920
EPS = 1e-6
TT = 512

R_CW = 0
R_CB = 16
R_BRG = 20
R_BIG = 24
R_LAM = 28
R_BA = 32
R_BL = 36
R_QG = 40
R_KG = 41
R_SK = 42
NVROW = 42
C_CL = 46
C_CL2 = 50
C_SE = 54
NVCOL = 58


class Tl:
    def __init__(self, t, name):
        self.t = t
        self.name = name
        self.w = None
        self.r = {}
        self.dsem = None
        self.dcnt = 0
        self.excl = False

    def __getitem__(self, k):
        return self.t[k]


class Eng:
    def __init__(self, name):
        self.name = name
        self.ops = []
        self.sem = None
        self.cnt = 0
        self.waited = {}


class KB:
    def __init__(self, nc, es):
        self.nc = nc
        self.es = es
        self.E = {n: Eng(n) for n in ("pe", "act", "dve", "pool", "sp")}
        self.sems = {}
        for n in ("pe", "act", "dve", "pool"):
            h = es.enter_context(nc.semaphore("s_" + n))
            self.E[n].sem = h
            self.sems[n] = h
        self.nsem = 4
        self.dtiles = []
        self.uid = 0
        self.psum_banks = []
        self.pi = 0
        self.ppi = {}

    def sb(self, name, shape, dt):
        return Tl(self.nc.alloc_sbuf_tensor(name, list(shape), dt), name)

    def dram(self, name, shape, dt, kind="Internal"):
        return Tl(self.nc.dram_tensor(name, list(shape), dt, kind=kind), name)

    def init_psum(self):
        for i in range(8):
            self.psum_banks.append(Tl(self.nc.alloc_psum_tensor("psb%d" % i, [128, 512], F32), "psb%d" % i))
            self.psum_banks[-1].excl = True

    def psum(self, pool="F"):
        base, n = (0, NPF) if pool == "F" else (NPF, 8 - NPF)
        i = self.ppi.get(pool, 0)
        self.ppi[pool] = i + 1
        return self.psum_banks[base + i % n]

    def _dsem(self, t):
        if t.dsem is None:
            name = "d_%s_%d" % (t.name[:12], self.nsem)
            t.dsem = self.es.enter_context(self.nc.semaphore(name))
            t.dkey = name
            self.sems[name] = t.dsem
            self.nsem += 1
            self.dtiles.append(t)
        return t.dkey

    def _waits(self, E, reads, writes, is_dma, skipkey=None):
        need = {}

        def nd(s, v):
            if s == skipkey:
                return
            if v > need.get(s, 0):
                need[s] = v
        for t in reads:
            if t.w is not None:
                nd(*t.w)
            if t.excl:
                for s, v in t.r.items():
                    if s != E.name:
                        nd(s, v)
        for t in writes:
            if t.w is not None and (is_dma or t.w[0] != E.name):
                nd(*t.w)
            for s, v in t.r.items():
                if is_dma or s != E.name:
                    nd(s, v)
        wl = [(s, v) for s, v in need.items() if E.waited.get(s, 0) < v]
        for s, v in wl:
            E.waited[s] = v
        return wl

    def op(self, eng, fn, reads=(), writes=(), inc=True):
        E = self.E[eng]
        wl = self._waits(E, reads, writes, False)
        val = E.cnt + 1
        if inc:
            E.cnt = val
        E.ops.append((wl, fn, (E.name, 1) if inc else None))
        for t in reads:
            if t.r.get(E.name, 0) < val:
                t.r[E.name] = val
        for t in writes:
            t.w = (E.name, val)
            t.r = {}

    def dma(self, q, out_ap, in_ap, reads=(), writes=(), semt=None, fill=False, nc_ok=False):
        E = self.E[q]
        key = self._dsem(semt)
        wl = self._waits(E, reads, writes, True, skipkey=(key if fill else None))
        semt.dcnt += 16
        val = semt.dcnt

        def fn(e, out_ap=out_ap, in_ap=in_ap, nc_ok=nc_ok):
            if nc_ok:
                return e.dma_start(out=out_ap, in_=in_ap, allow_slow_non_contiguous=True)
            return e.dma_start(out=out_ap, in_=in_ap)
        E.ops.append((wl, fn, (key, 16)))
        for t in reads:
            if t.r.get(key, 0) < val:
                t.r[key] = val
        for t in writes:
            t.w = (key, val)
            if not fill:
                t.r = {}

    def finish(self):
        E = self.E["sp"]
        wl = []
        for t in self.dtiles:
            if E.waited.get(t.dkey, 0) < t.dcnt:
                wl.append((t.dkey, t.dcnt))
        for n in ("pe", "act", "dve", "pool"):
            if self.E[n].cnt > 0:
                wl.append((n, self.E[n].cnt))
        E.ops.append((wl, None, None))

    def replay(self, block):
        sems = self.sems

        def mk(name):
            E = self.E[name]

            def body(e):
                for wl, fn, inc in E.ops:
                    for s, v in wl:
                        e.wait_ge(sems[s], v)
                    if fn is None:
                        continue
                    ins = fn(e)
                    if inc is not None:
                        ins.then_inc(sems[inc[0]], inc[1])
            return body
        block.sync(mk("sp"))
        block.gpsimd(mk("pool"))
        block.scalar(mk("act"))
        block.vector(mk("dve"))
        block.tensor(mk("pe"))


class Ring:
    def __init__(self, k, name, shape, dt, n):
        self.b = [k.sb("%s%d" % (name, i), shape, dt) for i in range(n)]
        self.i = 0

    def next(self):
        t = self.b[self.i % len(self.b)]
        self.i += 1
        return t


def build_program(SEQ=SEQ, NSTAGES=6, WITH_SAMPLE=True, PB=PB):
    NROW = PB + SB
    nc = bass.Bass("TRN2", target_bir_lowering=False)
    es = ExitStack()
    k = KB(nc, es)
    k.init_psum()

    def din(name, shape):
        return k.dram(name, shape, F32, kind="ExternalInput")

    def dout(name, shape):
        return k.dram(name, shape, F32, kind="ExternalOutput")

    xp = din("xp", [PB, SEQ, D]); xs = din("xs", [SB * SL, D])
    ck = din("ck", [DEPTH, SB, 128, 128]); cv = din("cv", [DEPTH, SB, 128, 128])
    stc = din("stc", [DEPTH, SB * 3, 512]); stl = din("stl", [DEPTH, SB, 512])
    crow = din("crow", [NROW, D])
    w_ada = din("w_ada", [DEPTH, D, 9 * D]); b_ada = din("b_ada", [DEPTH, 72, 128])
    w1g = din("w1g", [DEPTH, D, DFF]); w1u = din("w1u", [DEPTH, D, DFF]); w1d = din("w1d", [DEPTH, DFF, D])
    w_in = din("w_in", [DEPTH, D, 1792])
    q_gain = din("q_gain", [DEPTH, 64]); k_gain = din("k_gain", [DEPTH, 64]); sinks = din("sinks", [DEPTH, 8])
    conv_w = din("conv_w", [DEPTH, 16, 128]); conv_b = din("conv_b", [DEPTH, 4, 128])
    w_rg = din("w_rg", [DEPTH, 8, 64, 64]); b_rg = din("b_rg", [DEPTH, 4, 128])
    w_ig = din("w_ig", [DEPTH, 8, 64, 64]); b_ig = din("b_ig", [DEPTH, 4, 128])
    lam = din("lam", [DEPTH, 4, 128]); beta_a = din("beta_a", [DEPTH, 4, 128]); beta_l = din("beta_l", [DEPTH, 4, 128])
    w_out = din("w_out", [DEPTH, D, D])
    w2g = din("w2g", [DEPTH, D, DFF]); w2u = din("w2u", [DEPTH, D, DFF]); w2d = din("w2d", [DEPTH, DFF, D])
    c_ident = din("c_ident", [128, 128]); c_mprev = din("c_mprev", [128, 8 * 128]); c_mcur = din("c_mcur", [128, 8 * 128])

    yp = dout("yp", [PB, SEQ, D]); ys = dout("ys", [SB * SL, D])
    kpo = dout("kpo", [DEPTH, PB, 128, 128]); vpo = dout("vpo", [DEPTH, PB, 128, 128])
    cpo = dout("cpo", [DEPTH, PB, 3, 512]); hpo = dout("hpo", [DEPTH, PB, 512])
    kso = dout("kso", [DEPTH, SB, 128, 128]); vso = dout("vso", [DEPTH, SB, 128, 128])
    cso = dout("cso", [DEPTH, SB * 3, 512]); hso = dout("hso", [DEPTH, SB, 512])

    S = {}
    for l in range(DEPTH):
        S[("g", 1, l)] = k.dram("s_w1g%d" % l, [D, DFF], BF16)
        S[("u", 1, l)] = k.dram("s_w1u%d" % l, [D, DFF], BF16)
        S[("d", 1, l)] = k.dram("s_w1d%d" % l, [DFF, D], BF16)
        S[("g", 2, l)] = k.dram("s_w2g%d" % l, [D, DFF], BF16)
        S[("u", 2, l)] = k.dram("s_w2u%d" % l, [D, DFF], BF16)
        S[("d", 2, l)] = k.dram("s_w2d%d" % l, [DFF, D], BF16)
        S[("in", l)] = k.dram("s_win%d" % l, [D, WIN_COLS], BF16)
        S[("out", l)] = k.dram("s_wout%d" % l, [D, D], BF16)
    SRC = {("g", 1): w1g, ("u", 1): w1u, ("d", 1): w1d, ("g", 2): w2g, ("u", 2): w2u, ("d", 2): w2d}

    ident = k.sb("ident", [128, 128], F32)
    mprev = k.sb("mprev", [128, 8, 128], BF16)
    mcur = k.sb("mcur", [128, 8, 128], BF16)
    ones_bf = k.sb("ones_bf", [128, 128], BF16)
    blk64 = k.sb("blk64", [128, 128], BF16)
    ones_e = k.sb("ones_e", [128, 128], BF16)
    ones_o = k.sb("ones_o", [128, 128], BF16)
    consts = k.sb("constsem", [1, 4], F32)
    modT = [k.sb("modT%d" % l, [128, 72, NROW], F32) for l in range(DEPTH)]
    bT = [k.sb("bT%d" % l, [128, 72], F32) for l in range(DEPTH)]
    vecT = [k.sb("vecT%d" % l, [128, NVCOL], F32) for l in range(DEPTH)]
    wgate = [k.sb("wgate%d" % l, [128, 2, 4, 128], BF16) for l in range(DEPTH)]
    cT = k.sb("cT", [128, 8, NROW], BF16)

    xTs = [k.sb("xT%d" % i, [128, NCH, TT], F32) for i in range(2)]
    hTs = [k.sb("hT%d" % i, [128, NCH, TT], BF16) for i in range(2)]
    rstds = {"F": k.sb("rstdF", [128, TT], F32), "M": k.sb("rstdM", [128, TT], F32)}
    actT = k.sb("actT", [128, NF, TT], BF16)
    mq = k.sb("mq", [128, NCH, TT], BF16)
    knf = k.sb("knf", [128, 128], F32)
    xr = k.sb("xr", [128, 4, TT + 3], F32)
    ggr = k.sb("ggr", [128, 4, TT], F32)
    attnT = k.sb("attnT", [128, 4, TT], F32)
    tmps = {"F": Ring(k, "tmpF", [128, TT], F32, 2), "M": Ring(k, "tmpM", [128, TT], F32, 4)}
    xcbufs = Ring(k, "xcb", [128, TT], F32, 2)
    tmpb = Ring(k, "tmpb", [128, TT], BF16, 2)
    Ering = Ring(k, "E", [128, 4, 128], BF16, 6)
    vring = Ring(k, "vpad", [128, 4, 128], BF16, 6)
    WAs = {"F": Ring(k, "WAF", [128, 8, 256], BF16, 4), "M": Ring(k, "WAM", [128, 8, 256], BF16, 2)}
    WBs = {"F": Ring(k, "WBF", [128, 4, 256], BF16, 3), "M": Ring(k, "WBM", [128, 4, 256], BF16, 2)}
    kcarry = [k.sb("kcarry%d" % l, [128, 4, 128], BF16) for l in range(DEPTH)]
    vcarry = [k.sb("vcarry%d" % l, [128, 4, 128], BF16) for l in range(DEPTH)]
    ctail = [k.sb("ctail%d" % l, [128, 4, 3], F32) for l in range(DEPTH)]
    hstate = [k.sb("hstate%d" % l, [128, 4], F32) for l in range(DEPTH)]
    small = Ring(k, "small", [128, 512], F32, 4)
    ckr = Ring(k, "ckr", [128, 2, 2, 64], F32, 1)
    cvr = Ring(k, "cvr", [128, 128], F32, 1)
    kps = Ring(k, "kps", [128, 4, 128], BF16, 1)
    h0s = k.sb("h0s", [128, 4, SB], F32)
    cachecp = k.sb("cachecp", [1, 4], F32)

    def A(eng, fn, reads, writes, inc=True):
        k.op(eng, fn, reads, writes, inc)

    def act(out_ap, in_ap, func, reads, writes, bias=None, scale=None):
        kw = {}
        if bias is not None:
            kw["bias"] = bias
        if scale is not None:
            kw["scale"] = scale
        A("act", lambda e: e.activation(out=out_ap, in_=in_ap, func=func, **kw), reads, writes)

    def tt(eng, out_ap, in0, in1, op, reads, writes):
        A(eng, lambda e: e.tensor_tensor(out=out_ap, in0=in0, in1=in1, op=op), reads, writes)

    def ts(eng, out_ap, in0, s1, s2, op0, op1, reads, writes):
        if s2 is None:
            A(eng, lambda e: e.tensor_scalar(out=out_ap, in0=in0, scalar1=s1, scalar2=None, op0=op0), reads, writes)
        else:
            A(eng, lambda e: e.tensor_scalar(out=out_ap, in0=in0, scalar1=s1, scalar2=s2, op0=op0, op1=op1), reads, writes)

    def stt(eng, out_ap, in0, scalar, in1, op0, op1, reads, writes):
        A(eng, lambda e: e.scalar_tensor_tensor(out=out_ap, in0=in0, scalar=scalar, in1=in1, op0=op0, op1=op1), reads, writes)

    def cp(eng, out_ap, in_ap, reads, writes):
        if eng == "act":
            A("act", lambda e: e.copy(out=out_ap, in_=in_ap), reads, writes)
        else:
            A(eng, lambda e: e.tensor_copy(out=out_ap, in_=in_ap), reads, writes)

    def mm(out_ap, lhsT, rhs, start, stop, reads, writes, last):
        A("pe", lambda e: e.matmul(out_ap, lhsT=lhsT, rhs=rhs, start=start, stop=stop), reads, writes, inc=last)

    def tr(out_ap, in_ap, kk, reads, writes, last):
        A("pe", lambda e: e.transpose(out_ap, in_ap, ident.t[0:kk, 0:kk]), list(reads) + [ident], writes, inc=last)

    def memset(eng, ap, val, writes):
        A(eng, lambda e: e.memset(ap, val), [], writes)

    k.dma("sp", ident.t[:, :], c_ident.t[:, :], [], [ident], semt=consts, fill=True)
    k.dma("pool", mprev.t[:, :, :], c_mprev.t.rearrange("p (h q) -> p h q", h=8), [], [mprev], semt=mprev)
    k.dma("pool", mcur.t[:, :, :], c_mcur.t.rearrange("p (h q) -> p h q", h=8), [], [mcur], semt=mcur)
    memset("pool", ones_bf.t[:, :], 1.0, [ones_bf])
    memset("pool", blk64.t[:, :], 0.0, [blk64])
    memset("pool", blk64.t[0:64, 0:64], 1.0, [blk64])
    memset("pool", blk64.t[64:128, 64:128], 1.0, [blk64])
    memset("pool", ones_e.t[:, :], 0.0, [ones_e])
    memset("pool", ones_e.t[:, 0:64], 1.0, [ones_e])
    memset("pool", ones_o.t[:, :], 0.0, [ones_o])
    memset("pool", ones_o.t[:, 64:128], 1.0, [ones_o])
    for t_ in vring.b + kps.b + [mq]:
        memset("pool", t_.t[:, :, :], 0.0, [t_])
    for l in range(DEPTH):
        memset("pool", vcarry[l].t[:, :, :], 0.0, [vcarry[l]])
        memset("dve", wgate[l].t[:, :, :, :], 0.0, [wgate[l]])

    vrow = []
    for l in range(DEPTH):
        v = small.next()
        vrow.append(v)

        def ld(r0, n, src_ap, l=l, v=v):
            k.dma("sp", v.t[r0:r0 + n, 0:128], src_ap, [], [v], semt=v, fill=True)
        ld(R_CW, 16, conv_w.t[l])
        ld(R_CB, 4, conv_b.t[l]); ld(R_BRG, 4, b_rg.t[l]); ld(R_BIG, 4, b_ig.t[l])
        ld(R_LAM, 4, lam.t[l]); ld(R_BA, 4, beta_a.t[l]); ld(R_BL, 4, beta_l.t[l])
        for hh in range(2):
            k.dma("sp", v.t[R_QG:R_QG + 1, hh * 64:(hh + 1) * 64], q_gain.t[l:l + 1, :], [], [v], semt=v, fill=True)
            k.dma("sp", v.t[R_KG:R_KG + 1, hh * 64:(hh + 1) * 64], k_gain.t[l:l + 1, :], [], [v], semt=v, fill=True)
        for gi_, wsrc in enumerate((w_rg, w_ig)):
            for c in range(4):
                for par in range(2):
                    k.dma("pool", wgate[l].t[par * 64:(par + 1) * 64, gi_, c, par * 64:(par + 1) * 64],
                          wsrc.t[l, 2 * c + par], [], [wgate[l]], semt=wgate[l], fill=True)

    for l in range(DEPTH):
        ps = k.psum()
        tr(ps.t[:, 0:NVROW], vrow[l].t[0:NVROW, 0:128], NVROW, [vrow[l]], [ps], True)
        cp("dve", vecT[l].t[:, 0:NVROW], ps.t[:, 0:NVROW], [ps], [vecT[l]])
        for par in range(2):
            k.dma("sp", vecT[l].t[par * 64:(par + 1) * 64, R_SK:R_SK + 4], sinks.t[l, par:8:2].partition_broadcast(64),
                  [], [vecT[l]], semt=vecT[l], nc_ok=True)
        brow = small.next()
        k.dma("sp", brow.t[0:72, 0:128], b_ada.t[l], [], [brow], semt=brow)
        ps = k.psum()
        tr(ps.t[:, 0:72], brow.t[0:72, 0:128], 72, [brow], [ps], True)
        cp("dve", bT[l].t[:, :], ps.t[:, 0:72], [ps], [bT[l]])
        t1 = tmps["M"].next()
        act(t1.t[:, 0:4], vecT[l].t[:, R_LAM:R_LAM + 4], AF.Exp, [vecT[l]], [t1], scale=-1.0)
        act(t1.t[:, 0:4], t1.t[:, 0:4], AF.Ln, [t1], [t1], bias=1.0)
        ts("dve", vecT[l].t[:, C_CL:C_CL + 4], t1.t[:, 0:4], -8.0, None, ALU.mult, None, [t1, vecT[l]], [vecT[l]])
        ts("dve", vecT[l].t[:, C_CL2:C_CL2 + 4], t1.t[:, 0:4], -16.0, None, ALU.mult, None, [t1, vecT[l]], [vecT[l]])
        act(vecT[l].t[:, C_SE:C_SE + 4], vecT[l].t[:, R_SK:R_SK + 4], AF.Exp, [vecT[l]], [vecT[l]])

    t1 = xTs[1]
    k.dma("sp", t1.t[0:NROW, 0:2, :].rearrange("p a b -> p (a b)"), crow.t[:, :], [], [t1], semt=t1)
    act(t1.t[0:NROW, 0:2, :], t1.t[0:NROW, 0:2, :], AF.Silu, [t1], [t1])
    ps = k.psum()
    for kc in range(8):
        tr(ps.t[:, kc * NROW:(kc + 1) * NROW], t1.t[0:NROW, kc // 4, (kc % 4) * 128:(kc % 4 + 1) * 128], NROW, [t1], [ps], kc == 7)
    cp("dve", cT.t[:, :, :], ps.t[:, 0:8 * NROW].rearrange("p (a b) -> p a b", a=8), [ps], [cT])

    def convert_plain(dst, src_ap, rows, cols):
        nb = 4
        rb = rows // nb
        for i in range(nb):
            k.dma("pool", dst.t[i * rb:(i + 1) * rb, :], src_ap[i * rb:(i + 1) * rb, :], [], [dst], semt=dst, fill=True)

    CST = {}

    def staging():
        if "A" not in CST:
            def al(parent, h, nm):
                t_ = Tl(parent.t[:, 4 * h:4 * h + 4, :].rearrange("p a b -> p (a b)"), nm)
                t_.w = parent.w
                t_.r = dict(parent.r)
                return t_
            CST["A"] = [al(xTs[1], h, "cstA%d" % h) for h in range(2)]
            CST["B"] = [al(hTs[1], h, "cstB%d" % h) for h in range(2)]
            CST["n"] = 0
        return CST["A"], CST["B"]

    def merge_staging():
        if "A" not in CST:
            return
        for parent, als in ((xTs[1], CST["A"]), (hTs[1], CST["B"])):
            for al_ in als:
                for s_, v_ in al_.r.items():
                    parent.r[s_] = max(parent.r.get(s_, 0), v_)
                if al_.w is not None:
                    parent.r[al_.w[0]] = max(parent.r.get(al_.w[0], 0), al_.w[1])

    def convert_engine(dst, src_ap, rows, cols):
        if not ENGCONV:
            convert_plain(dst, src_ap, rows, cols)
            return
        stA, stB = staging()
        CH = 128 * 2048
        n = rows * cols // CH
        assert n * CH == rows * cols
        srcf = src_ap.rearrange("r c -> (r c)")
        dstf = dst.t.rearrange("r c -> (r c)")
        base = CST["n"]
        CST["n"] += n

        def load(i):
            a = stA[(base + i) % 2]
            k.dma("act", a.t[:, :], srcf[i * CH:(i + 1) * CH].rearrange("(p j) -> p j", j=2048), [], [a], semt=a)
        load(0)
        if n > 1:
            load(1)
        for i in range(n):
            a = stA[(base + i) % 2]
            b_ = stB[(base + i) % 2]
            cp("act" if (base + i) % 2 == 0 else "dve", b_.t[:, :], a.t[:, :], [a], [b_])
            k.dma("act", dstf[i * CH:(i + 1) * CH].rearrange("(p j) -> p j", j=2048), b_.t[:, :], [b_], [dst], semt=dst, fill=True)
            if i + 2 < n:
                load(i + 2)
            merge_staging()
            yield CONVC
        merge_staging()

    def convert_ffn(which, l):
        convert_plain(S[("u", which, l)], SRC[("u", which)].t[l], D, DFF)
        if which == 2:
            convert_plain(S[("d", which, l)], SRC[("d", which)].t[l], DFF, D)
        yield from convert_engine(S[("g", which, l)], SRC[("g", which)].t[l], D, DFF)
        if which == 1:
            yield from convert_engine(S[("d", which, l)], SRC[("d", which)].t[l], DFF, D)

    def convert_mixer(l):
        dst = S[("in", l)]
        src = w_in.t[l]

        def c1(d0, s0, n):
            k.dma("pool", dst.t[:, d0:d0 + n], src[:, s0:s0 + n], [], [dst], semt=dst, fill=True)
        c1(0, 0, 512)
        for g in range(2):
            for rep in range(2):
                c1(512 + g * 128 + rep * 64, 512 + g * 64, 64)
        c1(768, 768, 1024)
        c1(1792, 640, 128)
        yield from convert_engine(S[("out", l)], w_out.t[l], D, D)

    def compute_mod(l, groups):
        for g in groups:
            ps = k.psum()
            for jj in range(8):
                wt = WAs["F"].next()
                c0 = g * D + jj * 128
                wv = wt.t.bitcast(F32)
                k.dma("sp", wv[:, :, :], w_ada.t[l][:, c0:c0 + 128].rearrange("(kc p) n -> p kc n", p=128), [], [wt], semt=wt)
                wb = WAs["F"].next()
                cp("act" if jj % 2 == 0 else "dve", wb.t[:, :, 0:128], wv[:, :, :], [wt], [wb])
                for kc in range(8):
                    mm(ps.t[:, jj * NROW:(jj + 1) * NROW], wb.t[:, kc, 0:128], cT.t[:, kc, :],
                       kc == 0, kc == 7, [wb, cT], [ps], last=(kc == 7))
            dst = modT[l].t[:, g * 8:(g + 1) * 8, :]
            tt("dve", dst, ps.t[:, 0:8 * NROW].rearrange("p (a b) -> p a b", a=8),
               bT[l].t[:, g * 8:(g + 1) * 8].unsqueeze(2).to_broadcast([128, 8, NROW]), ALU.add, [ps, bT[l], modT[l]], [modT[l]])
            if g in (1, 4, 7):
                ts("dve", dst, dst, 1.0, None, ALU.add, None, [modT[l]], [modT[l]])
            if g in (2, 8):
                ts("dve", dst, dst, 0.5, None, ALU.mult, None, [modT[l]], [modT[l]])

    class Ctx:
        pass

    def set_stream(cx, st):
        cx.st = st
        cx.tmp = tmps[st]
        cx.WA = WAs[st]
        cx.WB = WBs[st]
        cx.rstd = rstds[st]

    def ps_(cx):
        return k.psum(cx.st)

    def mod_sc(cx, l, g, dc):
        return modT[l].t[:, g * 8 + dc, cx.p:cx.p + 1]

    def mod_bc(cx, l, g, dc):
        return modT[l].t[:, g * 8 + dc, PB:PB + SB].unsqueeze(2).to_broadcast([128, SB, SL])

    def v3(ap):
        return ap.rearrange("p (b i) -> p b i", i=SL)

    def rstd_from_psum(cx, ps, T, n):
        rstd = cx.rstd
        ts("dve", rstd.t[:, 0:T], ps.t[:, 0:T], 1.0 / n, EPS, ALU.mult, ALU.add, [ps], [rstd])
        act(rstd.t[:, 0:T], rstd.t[:, 0:T], AF.Ln, [rstd], [rstd])
        act(rstd.t[:, 0:T], rstd.t[:, 0:T], AF.Exp, [rstd], [rstd], scale=-0.5)

    def norm_mod(cx, l, gsh, gsc, sq):
        T = cx.T
        xT = cx.xT
        hT = cx.hT
        rstd = cx.rstd
        sq = hT
        ps = ps_(cx)
        for kc in range(8):
            mm(ps.t[:, 0:T], ones_bf.t[:, :], sq.t[:, kc, 0:T], kc == 0, kc == 7, [ones_bf, sq], [ps], last=(kc == 7))
        rstd_from_psum(cx, ps, T, float(D))
        for dc in range(8):
            t1 = cx.tmp.next()
            if cx.kind == "p":
                eng = "dve" if dc % 2 == 0 else "pool"
                tt(eng, t1.t[:, 0:T], xT.t[:, dc, 0:T], rstd.t[:, 0:T], ALU.mult, [xT, rstd], [t1])
                act(hT.t[:, dc, 0:T], t1.t[:, 0:T], AF.Identity, [t1, modT[l]], [hT],
                    bias=mod_sc(cx, l, gsh, dc), scale=mod_sc(cx, l, gsc, dc))
            else:
                tt("dve", t1.t[:, 0:T], xT.t[:, dc, 0:T], rstd.t[:, 0:T], ALU.mult, [xT, rstd], [t1])
                tt("dve", v3(t1.t[:, 0:T]), v3(t1.t[:, 0:T]), mod_bc(cx, l, gsc, dc), ALU.mult, [t1, modT[l]], [t1])
                tt("pool", v3(hT.t[:, dc, 0:T]), v3(t1.t[:, 0:T]), mod_bc(cx, l, gsh, dc), ALU.add, [t1, modT[l]], [hT])
        yield 7.0

    def resid_update(cx, l, gg, dc, py):
        T = cx.T
        xT = cx.xT
        if cx.kind == "p":
            stt("dve", xT.t[:, dc, 0:T], py.t[:, 0:T], mod_sc(cx, l, gg, dc), xT.t[:, dc, 0:T], ALU.mult, ALU.add,
                [py, modT[l], xT], [xT])
        else:
            t1 = cx.tmp.next()
            tt("dve", v3(t1.t[:, 0:T]), v3(py.t[:, 0:T]), mod_bc(cx, l, gg, dc), ALU.mult, [py, modT[l]], [t1])
            tt("pool", xT.t[:, dc, 0:T], xT.t[:, dc, 0:T], t1.t[:, 0:T], ALU.add, [xT, t1], [xT])

    def proj_out(cx, l, gg, src_tile, nk, wsrc):
        T = cx.T
        q = cx.st
        dq_ = "sp" if q == "F" else MQ
        for qp in range(4):
            pys = [ps_(cx) for _ in range(2)]
            for k0 in range(0, nk, 4):
                nj = min(4, nk - k0)
                wt = cx.WB.next()
                k.dma(dq_, wt.t[:, 0:nj, :],
                      wsrc.t[k0 * 128:(k0 + nj) * 128, qp * 256:(qp + 1) * 256].rearrange("(j p) d -> p j d", p=128),
                      [wsrc], [wt], semt=wt)
                for j in range(nj):
                    kk = k0 + j
                    for dq in range(2):
                        mm(pys[dq].t[:, 0:T], wt.t[:, j, dq * 128:(dq + 1) * 128], src_tile.t[:, kk, 0:T],
                           kk == 0, kk == nk - 1, [wt, src_tile], [pys[dq]], last=(kk == nk - 1 or (j == nj - 1 and dq == 1)))
                yield 0.26 * nj * 2 * T / 512 + 0.3
            for dq in range(2):
                dc = qp * 2 + dq
                resid_update(cx, l, gg, dc, pys[dq])
                if cx.sqnext:
                    act(cx.hT.t[:, dc, 0:T], cx.xT.t[:, dc, 0:T], AF.Square, [cx.xT], [cx.hT])
            yield 0.5

    def ffn(cx, l, which):
        set_stream(cx, "F")
        T = cx.T
        hT = cx.hT
        g0 = 0 if which == 1 else 6
        sg = S[("g", which, l)]
        su = S[("u", which, l)]
        for u in range(NF // 2):
            wg = cx.WA.next()
            k.dma("sp", wg.t[:, :, :], sg.t[:, u * 256:(u + 1) * 256].rearrange("(kc p) f -> p kc f", p=128), [sg], [wg], semt=wg)
            wu = cx.WA.next()
            k.dma("sp", wu.t[:, :, :], su.t[:, u * 256:(u + 1) * 256].rearrange("(kc p) f -> p kc f", p=128), [su], [wu], semt=wu)
            for j in range(2):
                f = u * 2 + j
                pg = ps_(cx)
                for kc in range(8):
                    mm(pg.t[:, 0:T], wg.t[:, kc, j * 128:(j + 1) * 128], hT.t[:, kc, 0:T], kc == 0, kc == 7, [wg, hT], [pg], last=(kc == 7))
                pu = ps_(cx)
                for kc in range(8):
                    mm(pu.t[:, 0:T], wu.t[:, kc, j * 128:(j + 1) * 128], hT.t[:, kc, 0:T], kc == 0, kc == 7, [wu, hT], [pu], last=(kc == 7))
                t1 = cx.tmp.next()
                act(t1.t[:, 0:T], pg.t[:, 0:T], AF.Silu, [pg], [t1])
                tt("dve", actT.t[:, f, 0:T], t1.t[:, 0:T], pu.t[:, 0:T], ALU.mult, [t1, pu], [actT])
                yield 4.2 * T / 512 + 0.6
        yield from proj_out(cx, l, g0 + 2, actT, NF, S[("d", which, l)])

    def attn_block(cx, l, tok0, nq, kprev, vprev_t, vcur_t):
        has_prev = kprev is not None
        vl = vecT[l]
        Es = {}
        for g in range(2):
            srcs = []
            if has_prev:
                srcs.append(("p", 128))
            srcs.append(("c", nq))
            for kind, nk in srcs:
                ps = ps_(cx)
                for hh in range(4):
                    h = 4 * g + hh
                    if kind == "p":
                        kt, kch, kc0 = kprev
                        lhsT = kt.t[:, kch + 2 * g + (h % 2), kc0:kc0 + 128]
                        rd = [kt, mq]
                    else:
                        lhsT = mq.t[:, 4 + 2 * g + (h % 2), tok0:tok0 + nq]
                        rd = [mq]
                    mm(ps.t[0:nk, hh * 128:hh * 128 + nq], lhsT, mq.t[:, h // 2, tok0:tok0 + nq], True, True, rd, [ps], last=(hh == 3))
                psv = ps.t[0:nk, :].rearrange("p (h q) -> p h q", h=4)[:, :, 0:nq]
                t1 = cx.tmp.next()
                t1v = t1.t[0:nk, :].rearrange("p (h q) -> p h q", h=4)[:, :, 0:nq]
                act(t1v, psv, AF.Exp, [ps], [t1], scale=0.125)
                Et = Ering.next()
                mt = mprev if kind == "p" else mcur
                tt("pool", Et.t[0:nk, :, 0:nq], t1v, mt.t[0:nk, 4 * g:4 * g + 4, 0:nq], ALU.mult, [t1, mt], [Et])
                Es[(g, kind)] = (Et, nk)
        yield (7.0 if nq == 128 else 1.5) * MSC
        po = ps_(cx)
        pd = ps_(cx)
        for which_, pt in (("o", po), ("d", pd)):
            for c in range(4):
                g = c // 2
                lst = []
                for par in range(2):
                    hh = (2 * c + par) % 4
                    for kind in (("p", "c") if has_prev else ("c",)):
                        Et, nk = Es[(g, kind)]
                        if which_ == "o":
                            vt = vprev_t if kind == "p" else vcur_t
                            lhsT = vt.t[0:nk, 2 * g + par, :]
                            rd = [vt, Et]
                        else:
                            ot = ones_e if par == 0 else ones_o
                            lhsT = ot.t[0:nk, :]
                            rd = [ot, Et]
                        lst.append((lhsT, Et.t[0:nk, hh, 0:nq], rd))
                for i, (lhsT, rhs, rd) in enumerate(lst):
                    mm(pt.t[:, c * 128:c * 128 + nq], lhsT, rhs, i == 0, i == len(lst) - 1, rd, [pt], last=(i == len(lst) - 1))
        pov = po.t[:, :].rearrange("p (c q) -> p c q", c=4)[:, :, 0:nq]
        pdv = pd.t[:, :].rearrange("p (c q) -> p c q", c=4)[:, :, 0:nq]
        t2 = cx.tmp.next()
        t2v = t2.t[:, :].rearrange("p (c q) -> p c q", c=4)[:, :, 0:nq]
        tt("dve", t2v, pdv, vl.t[:, C_SE:C_SE + 4].unsqueeze(2).to_broadcast([128, 4, nq]), ALU.add, [pd, vl], [t2])
        A("dve", lambda e: e.reciprocal(out=t2v, in_=t2v), [t2], [t2])
        tt("dve", attnT.t[:, :, tok0:tok0 + nq], pov, t2v, ALU.mult, [po, t2], [attnT])
        yield 4.0 if nq == 128 else 1.5

    def mixer(cx, l):
        set_stream(cx, "M")
        T = cx.T
        hT = cx.hT
        vl = vecT[l]
        memset("pool", mq.t[64:128, 4:8:2, :], 0.0, [mq])
        memset("pool", mq.t[0:64, 5:8:2, :], 0.0, [mq])
        sw = S[("in", l)]

        def load_win(u, ncols=256):
            wt = cx.WA.next()
            k.dma(MQ, wt.t[:, :, 0:ncols], sw.t[:, u * 256:u * 256 + ncols].rearrange("(kc p) f -> p kc f", p=128), [sw], [wt], semt=wt)
            return wt

        def zchunk(wt, j):
            ps = ps_(cx)
            for kc in range(8):
                mm(ps.t[:, 0:T], wt.t[:, kc, j * 128:(j + 1) * 128], hT.t[:, kc, 0:T], kc == 0, kc == 7, [wt, hT], [ps], last=(kc == 7))
            return ps

        def headnorm_a(ps):
            sq = tmpb.next()
            act(sq.t[:, 0:T], ps.t[:, 0:T], AF.Square, [ps], [sq])
            return sq

        def headnorm_b(ps, sq):
            p2 = ps_(cx)
            mm(p2.t[:, 0:T], blk64.t[:, :], sq.t[:, 0:T], True, True, [blk64, sq], [p2], last=True)
            r = cx.tmp.next()
            ts("dve", r.t[:, 0:T], p2.t[:, 0:T], 1.0 / 64, EPS, ALU.mult, ALU.add, [p2], [r])
            act(r.t[:, 0:T], r.t[:, 0:T], AF.Ln, [r], [r])
            act(r.t[:, 0:T], r.t[:, 0:T], AF.Exp, [r], [r], scale=-0.5)
            return r

        need_kout = (cx.kind == "s") or cx.last
        for u in range(3):
            wt = load_win(u)
            for j in range(2):
                c = u * 2 + j
                ps = zchunk(wt, j)
                sq = headnorm_a(ps)
                yield (4.0 * T / 512 + 0.5) * MSC
                r = headnorm_b(ps, sq)
                if c < 4:
                    stt("dve", mq.t[:, c, 0:T], ps.t[:, 0:T], vl.t[:, R_QG:R_QG + 1], r.t[:, 0:T], ALU.mult, ALU.mult, [ps, vl, r], [mq])
                else:
                    g = c - 4
                    for par in range(2):
                        lo, hi = par * 64, (par + 1) * 64
                        stt("dve", mq.t[lo:hi, 4 + 2 * g + par, 0:T], ps.t[lo:hi, 0:T], vl.t[lo:hi, R_KG:R_KG + 1], r.t[lo:hi, 0:T],
                            ALU.mult, ALU.mult, [ps, vl, r], [mq])
                    if need_kout:
                        lo, hi = g * 64, (g + 1) * 64
                        stt("dve", knf.t[lo:hi, :], ps.t[lo:hi, T - 128:T], vl.t[lo:hi, R_KG:R_KG + 1], r.t[lo:hi, T - 128:T],
                            ALU.mult, ALU.mult, [ps, vl, r], [knf])
                yield 1.0
        if cx.kind == "p":
            xrv = lambda c, off, n: xr.t[:, c, off:off + n]
            cp("pool", xr.t[:, :, 0:3], ctail[l].t[:, :, :], [ctail[l]], [xr])
        else:
            xr4 = lambda c: xr.t[:, c, 0:SB * 11].rearrange("p (b i) -> p b i", i=11)
            xrv = lambda c, off, n: xr4(c)[:, :, off:off + n]
            stc_sb = small.next()
            k.dma("sp", stc_sb.t[0:SB * 3, :], stc.t[l], [], [stc_sb], semt=stc_sb)
            ps = ps_(cx)
            for c in range(4):
                tr(ps.t[:, c * 48:(c + 1) * 48], stc_sb.t[0:SB * 3, c * 128:(c + 1) * 128], 48, [stc_sb], [ps], c == 3)
            for c in range(4):
                cp("dve", xrv(c, 0, 3), ps.t[:, c * 48:(c + 1) * 48].rearrange("p (b i) -> p b i", i=3), [ps], [xr])
        for u in range(2):
            wt = load_win(3 + u)
            for j in range(2):
                c = u * 2 + j
                ps = zchunk(wt, j)
                if cx.kind == "p":
                    cp("act", xrv(c, 3, T), ps.t[:, 0:T], [ps], [xr])
                else:
                    cp("act", xrv(c, 3, SL), v3(ps.t[:, 0:T]), [ps], [xr])
                yield 2.2 * T / 512 + 0.3
        for u in range(2):
            wt = load_win(5 + u)
            for j in range(2):
                c = u * 2 + j
                ps = zchunk(wt, j)
                act(ggr.t[:, c, 0:T], ps.t[:, 0:T], AF.Gelu_apprx_tanh, [ps], [ggr])
                yield 2.2 * T / 512 + 0.3
        wt = load_win(7, 128)
        nblk = T // 128 if cx.kind == "p" else SB
        nq = 128 if cx.kind == "p" else SL

        def vgroup(b0):
            vts = []
            ps = ps_(cx)
            for bb in range(4):
                blk = b0 + bb
                for kc in range(8):
                    mm(ps.t[0:nq, bb * 128:(bb + 1) * 128], hT.t[:, kc, blk * nq:(blk + 1) * nq], wt.t[:, kc, 0:128],
                       kc == 0, kc == 7, [hT, wt], [ps], last=(kc == 7))
            for bb in range(4):
                blk = b0 + bb
                vt = vring.next()
                src = ps.t[0:nq, bb * 128:(bb + 1) * 128].rearrange("p (g d) -> p g d", g=2)
                cp("act", vt.t[0:nq, 0:4:2, 0:64], src, [ps], [vt])
                cp("act", vt.t[0:nq, 1:4:2, 64:128], src, [ps], [vt])
                vts.append(vt)
                if cx.kind == "p" and cx.last and blk == nblk - 1:
                    st = small.next()
                    cp("act", st.t[:, 0:128], ps.t[:, bb * 128:(bb + 1) * 128], [ps], [st])
                    k.dma("pool", vpo.t[l, cx.p], st.t[:, 0:128], [st], [], semt=st)
            if cx.kind == "s":
                st = small.next()
                cp("act", st.t[0:SL, :], ps.t[0:SL, :], [ps], [st])
                k.dma("pool", vso.t[l, b0:b0 + 4, 120:128, :].rearrange("b i f -> i b f"),
                      st.t[0:SL, :].rearrange("p (b f) -> p b f", b=4), [st], [], semt=st)
            return vts

        def lru_a(c):
            xc = xcbufs.next()
            if cx.kind == "p":
                xcv = xc.t[:, 0:T]
                L = T
            else:
                xcv = v3(xc.t[:, 0:T])
                L = SL
            ts("dve", xcv, xrv(c, 0, L), vl.t[:, R_CW + c:R_CW + c + 1], vl.t[:, R_CB + c:R_CB + c + 1], ALU.mult, ALU.add, [xr, vl], [xc])
            for tap in range(1, 4):
                stt("dve", xcv, xrv(c, tap, L), vl.t[:, R_CW + tap * 4 + c:R_CW + tap * 4 + c + 1], xcv, ALU.mult, ALU.add, [xr, vl, xc], [xc])
            xcb = tmpb.next()
            cp("pool", xcb.t[:, 0:T], xc.t[:, 0:T], [xc], [xcb])
            return xc, xcb

        def lru_b(c, xc, xcb):
            st_ = lru_b1(c, xc, xcb)
            lru_b2(c, xc, st_)

        def lru_b1(c, xc, xcb):
            pr = ps_(cx)
            mm(pr.t[:, 0:T], wgate[l].t[:, 0, c, :], xcb.t[:, 0:T], True, True, [wgate[l], xcb], [pr], last=True)
            pi_ = ps_(cx)
            mm(pi_.t[:, 0:T], wgate[l].t[:, 1, c, :], xcb.t[:, 0:T], True, True, [wgate[l], xcb], [pi_], last=True)
            ra = cx.tmp.next()
            act(ra.t[:, 0:T], pr.t[:, 0:T], AF.Sigmoid, [pr, vl], [ra], bias=vl.t[:, R_BRG + c:R_BRG + c + 1])
            gi = cx.tmp.next()
            act(gi.t[:, 0:T], pi_.t[:, 0:T], AF.Sigmoid, [pi_, vl], [gi], bias=vl.t[:, R_BIG + c:R_BIG + c + 1])
            a2 = cx.tmp.next()
            act(a2.t[:, 0:T], ra.t[:, 0:T], AF.Exp, [ra, vl], [a2], scale=vl.t[:, C_CL2 + c:C_CL2 + c + 1])
            act(ra.t[:, 0:T], ra.t[:, 0:T], AF.Exp, [ra, vl], [ra], scale=vl.t[:, C_CL + c:C_CL + c + 1])
            act(a2.t[:, 0:T], a2.t[:, 0:T], AF.Sqrt, [a2], [a2], bias=1.0, scale=-1.0)
            return (ra, gi, a2)

        def lru_b2(c, xc, st_):
            ra, gi, a2 = st_
            tt("dve", gi.t[:, 0:T], gi.t[:, 0:T], xc.t[:, 0:T], ALU.mult, [gi, xc], [gi])
            tt("dve", gi.t[:, 0:T], gi.t[:, 0:T], a2.t[:, 0:T], ALU.mult, [gi, a2], [gi])
            hs = cx.tmp.next()
            if cx.kind == "p":
                A("dve", lambda e, hs=hs, ra=ra, gi=gi, c=c: e.tensor_tensor_scan(
                    out=hs.t[:, 0:T], data0=ra.t[:, 0:T], data1=gi.t[:, 0:T], initial=hstate[l].t[:, c:c + 1],
                    op0=ALU.mult, op1=ALU.add), [ra, gi, hstate[l]], [hs])
                cp("act", hstate[l].t[:, c:c + 1], hs.t[:, T - 1:T], [hs], [hstate[l]])
            else:
                a3 = v3(ra.t[:, 0:T])
                u3 = v3(gi.t[:, 0:T])
                t16 = small.next()
                tt("dve", t16.t[:, 0:SB], a3[:, :, 0], h0s.t[:, c, :], ALU.mult, [ra, h0s], [t16])
                tt("dve", u3[:, :, 0], u3[:, :, 0], t16.t[:, 0:SB], ALU.add, [gi, t16], [gi])
                memset("dve", a3[:, :, 0], 0.0, [ra])
                A("dve", lambda e, hs=hs, ra=ra, gi=gi: e.tensor_tensor_scan(
                    out=hs.t[:, 0:T], data0=ra.t[:, 0:T], data1=gi.t[:, 0:T], initial=0.0,
                    op0=ALU.mult, op1=ALU.add), [ra, gi], [hs])
                cp("act", h0s.t[:, c, :], v3(hs.t[:, 0:T])[:, :, SL - 1], [hs], [h0s])
            tt("pool", ggr.t[:, c, 0:T], hs.t[:, 0:T], ggr.t[:, c, 0:T], ALU.mult, [hs, ggr], [ggr])

        if cx.kind == "p":
            vtiles = vgroup(0)
            yield 2.0
            nxt = lru_a(0)
            for blk in range(nblk):
                xc, xcb = nxt
                st_ = lru_b1(blk, xc, xcb)
                if blk + 1 < nblk:
                    nxt = lru_a(blk + 1)
                yield 4.0 * MSC
                lru_b2(blk, xc, st_)
                yield 3.0 * MSC
                if blk == 0:
                    kp = None if cx.first else (kcarry[l], 0, 0)
                    vp = None if cx.first else vcarry[l]
                    yield from attn_block(cx, l, 0, 128, kp, vp, vtiles[0])
                else:
                    yield from attn_block(cx, l, blk * 128, 128, (mq, 4, (blk - 1) * 128), vtiles[blk - 1], vtiles[blk])
            if not cx.last:
                cp("pool", kcarry[l].t[:, :, :], mq.t[:, 4:8, T - 128:T], [mq], [kcarry[l]])
                cp("pool", vcarry[l].t[:, :, :], vtiles[nblk - 1].t[:, :, :], [vtiles[nblk - 1]], [vcarry[l]])
        else:
            for b0 in range(0, SB, 4):
                vts = vgroup(b0)
                for bb in range(4):
                    b = b0 + bb
                    ckt = ckr.next()
                    for rp in range(2):
                        k.dma("sp", ckt.t[:, :, rp, :], ck.t[l, b].rearrange("w (g d) -> w g d", g=2), [], [ckt], semt=ckt, fill=(rp == 1))
                    cvt = cvr.next()
                    k.dma("sp", cvt.t[:, :], cv.t[l, b], [], [cvt], semt=cvt)
                    ps = ps_(cx)
                    for g in range(2):
                        src = ckt.t[:, g, :, :].rearrange("p r d -> p (r d)")
                        tr(ps.t[:, g * 128:(g + 1) * 128], src, 128, [ckt], [ps], g == 1)
                    kpt = kps.next()
                    for par in range(2):
                        lo, hi = par * 64, (par + 1) * 64
                        cp("act", kpt.t[lo:hi, par:4:2, :], ps.t[lo:hi, 0:256].rearrange("p (g q) -> p g q", g=2), [ps], [kpt])
                    vpt = vring.next()
                    srcv = cvt.t[:, :].rearrange("p (g d) -> p g d", g=2)
                    cp("pool", vpt.t[:, 0:4:2, 0:64], srcv, [cvt], [vpt])
                    cp("pool", vpt.t[:, 1:4:2, 64:128], srcv, [cvt], [vpt])
                    yield from attn_block(cx, l, b * SL, SL, (kpt, 0, 0), vpt, vts[bb])
            for c in range(4):
                xc, xcb = lru_a(c)
                yield 2.0
                lru_b(c, xc, xcb)
                yield 5.0

        if need_kout:
            ps = ps_(cx)
            tr(ps.t[:, 0:128], knf.t[:, :], 128, [knf], [ps], True)
            st = small.next()
            cp("act", st.t[:, 0:128], ps.t[:, 0:128], [ps], [st])
            if cx.kind == "p":
                k.dma("pool", kpo.t[l, cx.p], st.t[:, 0:128], [st], [], semt=st)
            else:
                for b in range(SB):
                    k.dma("pool", kso.t[l, b, 120:128, :], st.t[b * SL:(b + 1) * SL, 0:128], [st], [], semt=st)
        if cx.kind == "p":
            cp("pool", ctail[l].t[:, :, :], xr.t[:, :, T:T + 3], [xr], [ctail[l]])
            if cx.last:
                ps = ps_(cx)
                for c in range(4):
                    tr(ps.t[0:3, c * 128:(c + 1) * 128], xr.t[:, c, T:T + 3], 128, [xr], [ps], c == 3)
                st = small.next()
                cp("act", st.t[0:3, :], ps.t[0:3, :], [ps], [st])
                k.dma("pool", cpo.t[l, cx.p], st.t[0:3, :], [st], [], semt=st)
                ps = ps_(cx)
                for c in range(4):
                    tr(ps.t[0:1, c * 128:(c + 1) * 128], hstate[l].t[:, c:c + 1], 128, [hstate[l]], [ps], c == 3)
                st = small.next()
                cp("act", st.t[0:1, :], ps.t[0:1, :], [ps], [st])
                k.dma("pool", hpo.t[l, cx.p:cx.p + 1, :], st.t[0:1, :], [st], [], semt=st)
        else:
            t48 = cx.tmp.next()
            for c in range(4):
                cp("dve", t48.t[:, c * 48:(c + 1) * 48].rearrange("p (b i) -> p b i", i=3), xrv(c, SL, 3), [xr], [t48])
            ps = ps_(cx)
            for c in range(4):
                tr(ps.t[0:48, c * 128:(c + 1) * 128], t48.t[:, c * 48:(c + 1) * 48], 128, [t48], [ps], c == 3)
            st = small.next()
            cp("act", st.t[0:48, :], ps.t[0:48, :], [ps], [st])
            k.dma("pool", cso.t[l], st.t[0:48, :], [st], [], semt=st)
            ps = ps_(cx)
            for c in range(4):
                tr(ps.t[0:SB, c * 128:(c + 1) * 128], h0s.t[:, c, :], 128, [h0s], [ps], c == 3)
            st = small.next()
            cp("act", st.t[0:SB, :], ps.t[0:SB, :], [ps], [st])
            k.dma("pool", hso.t[l], st.t[0:SB, :], [st], [], semt=st)
        yield 1.0

        rstd = cx.rstd
        for (srct, beta0, m0) in ((attnT, R_BA, 0), (ggr, R_BL, 4)):
            act(hT.t[:, 0:4, 0:T], srct.t[:, :, 0:T], AF.Square, [srct], [hT])
            yield 2.5
            ps = ps_(cx)
            for c in range(4):
                mm(ps.t[:, 0:T], ones_bf.t[:, :], hT.t[:, c, 0:T], c == 0, c == 3, [ones_bf, hT], [ps], last=(c == 3))
            rstd_from_psum(cx, ps, T, 512.0)
            for c in range(4):
                stt("dve", mq.t[:, m0 + c, 0:T], srct.t[:, c, 0:T], vl.t[:, beta0 + c:beta0 + c + 1], rstd.t[:, 0:T],
                    ALU.mult, ALU.mult, [srct, vl, rstd], [mq])
            yield 4.0
        yield from proj_out(cx, l, 5, mq, 8, S[("out", l)])

    def load_x(cx):
        set_stream(cx, "F")
        T = cx.T
        xT = cx.xT
        for blk in range(T // 128):
            for half in range(2):
                xin = small.next()
                if cx.kind == "p":
                    src = xp.t[cx.p, cx.tok0 + blk * 128:cx.tok0 + (blk + 1) * 128, half * 512:(half + 1) * 512]
                else:
                    src = xs.t[:, half * 512:(half + 1) * 512]
                k.dma("sp", xin.t[:, :], src, [], [xin], semt=xin)
                ps = ps_(cx)
                for q in range(4):
                    tr(ps.t[:, q * 128:(q + 1) * 128], xin.t[:, q * 128:(q + 1) * 128], 128, [xin], [ps], q == 3)
                cp("act" if half == 0 else "dve", xT.t[:, half * 4:half * 4 + 4, blk * 128:(blk + 1) * 128],
                   ps.t[:, :].rearrange("p (c q) -> p c q", c=4), [ps], [xT])
            yield 1.5
        act(cx.hT.t[:, :, 0:T], xT.t[:, :, 0:T], AF.Square, [xT], [cx.hT])

    def store_y(cx):
        set_stream(cx, "F")
        T = cx.T
        xT = cx.xT
        for blk in range(T // 128):
            for half in range(2):
                yo = small.next()
                ps = ps_(cx)
                for q in range(4):
                    dc = half * 4 + q
                    tr(ps.t[:, q * 128:(q + 1) * 128], xT.t[:, dc, blk * 128:(blk + 1) * 128], 128, [xT], [ps], q == 3)
                cp("act" if half == 0 else "dve", yo.t[:, :], ps.t[:, :], [ps], [yo])
                if cx.kind == "p":
                    dst = yp.t[cx.p, cx.tok0 + blk * 128:cx.tok0 + (blk + 1) * 128, half * 512:(half + 1) * 512]
                else:
                    dst = ys.t[:, half * 512:(half + 1) * 512]
                k.dma("pool", dst, yo.t[:, :], [yo], [], semt=yo)
            yield 1.5

    stage_list = []
    for l in range(DEPTH):
        stage_list += [("ffn", l, 1), ("mix", l, 0), ("ffn", l, 2)]
    stage_list = stage_list[:NSTAGES]
    NS = len(stage_list)

    def prefetch_for(si):
        if si >= NS:
            return
        kind, l, which = stage_list[si]
        if kind == "ffn":
            compute_mod(l, (0, 1, 2) if which == 1 else (6, 7, 8))
            yield from convert_ffn(which, l)
        else:
            compute_mod(l, (3, 4, 5))
            yield from convert_mixer(l)

    tiles = []
    if WITH_SAMPLE:
        cx = Ctx()
        cx.kind = "s"; cx.p = None; cx.T = SB * SL; cx.tok0 = 0; cx.first = True; cx.last = True
        tiles.append(cx)
    for p in range(PB):
        for tb in range(SEQ // TT):
            cx = Ctx()
            cx.kind = "p"; cx.p = p; cx.T = TT; cx.tok0 = tb * TT
            cx.first = (tb == 0); cx.last = (tb == SEQ // TT - 1)
            tiles.append(cx)
    for i, cx in enumerate(tiles):
        cx.idx = i
        cx.xT = xTs[i % 2]
        cx.hT = hTs[i % 2]

    for l in range(DEPTH):
        k.dma("pool", kso.t[l, :, 0:120, :], ck.t[l, :, 8:128, :], [], [], semt=cachecp, fill=True)
        k.dma("pool", vso.t[l, :, 0:120, :], cv.t[l, :, 8:128, :], [], [], semt=cachecp, fill=True)

    def norm_for(cx, si):
        if si >= NS:
            return
        kind, l, which = stage_list[si]
        g0 = 3 if kind == "mix" else (0 if which == 1 else 6)
        yield from norm_mod(cx, l, g0, g0 + 1, None)

    def tile_stage_gen(cx, si):
        for v_ in tile_stage_body(cx, si):
            yield v_
        if si < NS:
            yield from norm_for(cx, si + 1)

    def tile_stage_body(cx, si):
        if si == -1:
            if cx.kind == "p" and cx.first:
                for l in range(DEPTH):
                    memset("pool", ctail[l].t[:, :, :], 0.0, [ctail[l]])
                    memset("pool", hstate[l].t[:, :], 0.0, [hstate[l]])
            yield from load_x(cx)
        elif si == NS:
            yield from store_y(cx)
        else:
            kind, l, which = stage_list[si]
            cx.sqnext = (si + 1 < NS)
            if kind == "ffn":
                yield from ffn(cx, l, which)
            else:
                if cx.kind == "s":
                    set_stream(cx, "M")
                    stl_sb = small.next()
                    k.dma("sp", stl_sb.t[0:SB, :], stl.t[l], [], [stl_sb], semt=stl_sb)
                    ps = ps_(cx)
                    for c in range(4):
                        tr(ps.t[:, c * SB:(c + 1) * SB], stl_sb.t[0:SB, c * 128:(c + 1) * 128], SB, [stl_sb], [ps], c == 3)
                    cp("dve", h0s.t[:, :, :], ps.t[:, 0:4 * SB].rearrange("p (c b) -> p c b", c=4), [ps], [h0s])
                yield from mixer(cx, l)

    def drain(g):
        for _ in g:
            pass

    def run_pair(ga, gb):
        ta = tb = 0.0
        da = db = False
        while not (da and db):
            if db or (not da and ta <= tb):
                try:
                    ta += next(ga)
                except StopIteration:
                    da = True
            else:
                try:
                    tb += next(gb)
                except StopIteration:
                    db = True

    def stage_kind(si):
        if si < 0 or si >= NS:
            return "io"
        return stage_list[si][0]

    if NSTAGES > 0:
        compute_mod(0, (0, 1, 2))
        drain(convert_ffn(1, 0))
    ti = 0
    while ti < len(tiles):
        A_ = tiles[ti]
        B_ = tiles[ti + 1] if (PIPE and A_.kind == "p" and ti + 1 < len(tiles) and tiles[ti + 1].kind == "p") else None
        seqA = list(range(-1, NS + 1))
        if B_ is None:
            for si in seqA:
                g2 = None
                if ti == 0 and 0 <= si < NS:
                    g2 = prefetch_for(si + 1)
                    try:
                        next(g2)
                    except StopIteration:
                        g2 = None
                g1 = tile_stage_gen(A_, si)
                if g2 is not None:
                    run_pair(g1, g2)
                else:
                    drain(g1)
            ti += 1
            continue
        for s in range(-1, NS + 2):
            sa = s if s <= NS else None
            sb = s - 1 if (s - 1 >= -1 and s - 1 <= NS) else None
            if ti == 0 and sa is not None and 0 <= sa < NS:
                drain(prefetch_for(sa + 1))
            ga = tile_stage_gen(A_, sa) if sa is not None else None
            gb = tile_stage_gen(B_, sb) if sb is not None else None
            if ga is not None and gb is not None:
                ka, kb = stage_kind(sa), stage_kind(sb)
                if {ka, kb} == {"ffn", "mix"}:
                    run_pair(ga, gb)
                else:
                    drain(ga)
                    drain(gb)
            elif ga is not None:
                drain(ga)
            elif gb is not None:
                drain(gb)
        ti += 2

    k.finish()
    block = es.enter_context(nc.Block())
    k.replay(block)
    es.close()
    return nc


_CACHE = {}


def _consts():
    slopes = np.array([2.0 ** (-8.0 * (h + 1) / 8) for h in range(8)], dtype=np.float64)
    j = np.arange(128)[:, None, None]
    i = np.arange(128)[None, None, :]
    s = slopes[None, :, None]
    dprev = 128 + i - j
    mprev = np.where(j >= i, np.exp(-s * dprev), 0.0).astype(np.float32)
    dcur = i - j
    mcur = np.where(j <= i, np.exp(-s * dcur), 0.0).astype(np.float32)
    return (np.eye(128, dtype=np.float32), np.ascontiguousarray(mprev.reshape(128, 1024)),
            np.ascontiguousarray(mcur.reshape(128, 1024)))


def kernel(x_prompt, x_sample, cache_k, cache_v, state_conv, state_lru, c_prompt, c_sample,
           w_ada, b_ada, w1_gate, w1_up, w1_down, w_in, q_gain, k_gain, sinks,
           conv_w, conv_b, w_rg, b_rg, w_ig, b_ig, lru_lambda, beta_attn, beta_lru,
           w_out, w2_gate, w2_up, w2_down):
    f = lambda a: np.ascontiguousarray(np.asarray(a, dtype=np.float32))
    if "nc" not in _CACHE:
        _CACHE["nc"] = build_program()
    nc = _CACHE["nc"]
    ident, mprev, mcur = _consts()
    shared = {
        "w_ada": f(w_ada), "b_ada": f(b_ada).reshape(DEPTH, 72, 128),
        "w1g": f(w1_gate), "w1u": f(w1_up), "w1d": f(w1_down), "w_in": f(w_in),
        "q_gain": f(q_gain), "k_gain": f(k_gain), "sinks": f(sinks),
        "conv_w": f(conv_w).reshape(DEPTH, 16, 128), "conv_b": f(conv_b).reshape(DEPTH, 4, 128),
        "w_rg": f(w_rg), "b_rg": f(b_rg).reshape(DEPTH, 4, 128), "w_ig": f(w_ig), "b_ig": f(b_ig).reshape(DEPTH, 4, 128),
        "lam": f(lru_lambda).reshape(DEPTH, 4, 128), "beta_a": f(beta_attn).reshape(DEPTH, 4, 128),
        "beta_l": f(beta_lru).reshape(DEPTH, 4, 128), "w_out": f(w_out),
        "w2g": f(w2_gate), "w2u": f(w2_up), "w2d": f(w2_down),
        "c_ident": ident, "c_mprev": mprev, "c_mcur": mcur,
    }
    x_prompt = f(x_prompt); x_sample = f(x_sample); cache_k = f(cache_k); cache_v = f(cache_v)
    state_conv = f(state_conv); state_lru = f(state_lru); c_prompt = f(c_prompt); c_sample = f(c_sample)
    in_maps = []
    for c in range(NCORES):
        m = dict(shared)
        m["xp"] = x_prompt[PB * c:PB * (c + 1)]
        m["xs"] = x_sample[SB * c:SB * (c + 1)].reshape(SB * SL, D)
        m["ck"] = np.ascontiguousarray(cache_k[:, SB * c:SB * (c + 1)].reshape(DEPTH, SB, 128, 128))
        m["cv"] = np.ascontiguousarray(cache_v[:, SB * c:SB * (c + 1)].reshape(DEPTH, SB, 128, 128))
        m["stc"] = np.ascontiguousarray(state_conv[:, SB * c:SB * (c + 1)].reshape(DEPTH, SB * 3, 512))
        m["stl"] = np.ascontiguousarray(state_lru[:, SB * c:SB * (c + 1)])
        m["crow"] = np.ascontiguousarray(np.concatenate([c_prompt[PB * c:PB * (c + 1)], c_sample[SB * c:SB * (c + 1)]], axis=0))
        in_maps.append(m)
    res = run_bass_kernel_spmd(nc, in_maps, core_ids=list(range(NCORES)))
    R = res.results
    cat = lambda name, ax: np.concatenate([np.asarray(r[name]) for r in R], axis=ax)
    y_prompt = cat("yp", 0)
    y_sample = cat("ys", 0).reshape(NCORES * SB, SL, D)
    k_prompt = cat("kpo", 1).reshape(DEPTH, NCORES * PB, 128, 2, 64)
    v_prompt = cat("vpo", 1).reshape(DEPTH, NCORES * PB, 128, 2, 64)
    conv_prompt = cat("cpo", 1)
    lru_prompt = cat("hpo", 1)
    k_sample = cat("kso", 1).reshape(DEPTH, NCORES * SB, 128, 2, 64)
    v_sample = cat("vso", 1).reshape(DEPTH, NCORES * SB, 128, 2, 64)
    conv_sample = np.concatenate([np.asarray(r["cso"]).reshape(DEPTH, SB, 3, 512) for r in R], axis=1)
    lru_sample = cat("hso", 1)
    return (y_prompt, y_sample, k_prompt, v_prompt, conv_prompt, lru_prompt,
            k_sample, v_sample, conv_sample, lru_sample)
```

```python
import numpy as np
import os
MIXSTOP = float(os.environ.get('MIXSTOP', '99'))
ATTSTOP = int(os.environ.get('ATTSTOP', '99'))
PIPE = int(os.environ.get('PIPE', '1'))
NPF = int(os.environ.get('NPF', '4'))
MSC = float(os.environ.get('MSC', '1.0'))
MQ = os.environ.get('MQ', 'sp')
from contextlib import ExitStack
import concourse.bass as bass
import concourse.mybir as mybir
from concourse.bass_utils import run_bass_kernel_spmd

F32 = mybir.dt.float32
BF16 = mybir.dt.bfloat16
AF = mybir.ActivationFunctionType
ALU = mybir.AluOpType

NCORES = 8
D = 1024
NCH = 8
DFF = 2816
NF = 22
DEPTH = 2
SEQ = 2048
PB = 2
SB = 16
SL = 8
NROW = PB + SB
WIN_COLS = 1920
EPS = 1e-6
TT = 512

R_CW = 0
R_CB = 16
R_BRG = 20
R_BIG = 24
R_LAM = 28
R_BA = 32
R_BL = 36
R_QG = 40
R_KG = 41
R_SK = 42
NVROW = 42
C_CL = 46
C_CL2 = 50
C_SE = 54
NVCOL = 58


class Tl:
    def __init__(self, t, name):
        self.t = t
        self.name = name
        self.w = None
        self.r = {}
        self.dsem = None
        self.dcnt = 0
        self.excl = False

    def __getitem__(self, k):
        return self.t[k]


class Eng:
    def __init__(self, name):
        self.name = name
        self.ops = []
        self.sem = None
        self.cnt = 0
        self.waited = {}


class KB:
    def __init__(self, nc, es):
        self.nc = nc
        self.es = es
        self.E = {n: Eng(n) for n in ("pe", "act", "dve", "pool", "sp")}
        self.sems = {}
        for n in ("pe", "act", "dve", "pool"):
            h = es.enter_context(nc.semaphore("s_" + n))
            self.E[n].sem = h
            self.sems[n] = h
        self.nsem = 4
        self.dtiles = []
        self.uid = 0
        self.psum_banks = []
        self.pi = 0
        self.ppi = {}

    def sb(self, name, shape, dt):
        return Tl(self.nc.alloc_sbuf_tensor(name, list(shape), dt), name)

    def dram(self, name, shape, dt, kind="Internal"):
        return Tl(self.nc.dram_tensor(name, list(shape), dt, kind=kind), name)

    def init_psum(self):
        for i in range(8):
            self.psum_banks.append(Tl(self.nc.alloc_psum_tensor("psb%d" % i, [128, 512], F32), "psb%d" % i))
            self.psum_banks[-1].excl = True

    def psum(self, pool="F"):
        base, n = (0, NPF) if pool == "F" else (NPF, 8 - NPF)
        i = self.ppi.get(pool, 0)
        self.ppi[pool] = i + 1
        return self.psum_banks[base + i % n]

    def _dsem(self, t):
        if t.dsem is None:
            name = "d_%s_%d" % (t.name[:12], self.nsem)
            t.dsem = self.es.enter_context(self.nc.semaphore(name))
            t.dkey = name
            self.sems[name] = t.dsem
            self.nsem += 1
            self.dtiles.append(t)
        return t.dkey

    def _waits(self, E, reads, writes, is_dma, skipkey=None):
        need = {}

        def nd(s, v):
            if s == skipkey:
                return
            if v > need.get(s, 0):
                need[s] = v
        for t in reads:
            if t.w is not None:
                nd(*t.w)
            if t.excl:
                for s, v in t.r.items():
                    if s != E.name:
                        nd(s, v)
        for t in writes:
            if t.w is not None and (is_dma or t.w[0] != E.name):
                nd(*t.w)
            for s, v in t.r.items():
                if is_dma or s != E.name:
                    nd(s, v)
        wl = [(s, v) for s, v in need.items() if E.waited.get(s, 0) < v]
        for s, v in wl:
            E.waited[s] = v
        return wl

    def op(self, eng, fn, reads=(), writes=(), inc=True):
        E = self.E[eng]
        wl = self._waits(E, reads, writes, False)
        val = E.cnt + 1
        if inc:
            E.cnt = val
        E.ops.append((wl, fn, (E.name, 1) if inc else None))
        for t in reads:
            if t.r.get(E.name, 0) < val:
                t.r[E.name] = val
        for t in writes:
            t.w = (E.name, val)
            t.r = {}

    def dma(self, q, out_ap, in_ap, reads=(), writes=(), semt=None, fill=False, nc_ok=False):
        E = self.E[q]
        key = self._dsem(semt)
        wl = self._waits(E, reads, writes, True, skipkey=(key if fill else None))
        semt.dcnt += 16
        val = semt.dcnt

        def fn(e, out_ap=out_ap, in_ap=in_ap, nc_ok=nc_ok):
            if nc_ok:
                return e.dma_start(out=out_ap, in_=in_ap, allow_slow_non_contiguous=True)
            return e.dma_start(out=out_ap, in_=in_ap)
        E.ops.append((wl, fn, (key, 16)))
        for t in reads:
            if t.r.get(key, 0) < val:
                t.r[key] = val
        for t in writes:
            t.w = (key, val)
            if not fill:
                t.r = {}

    def finish(self):
        E = self.E["sp"]
        wl = []
        for t in self.dtiles:
            if E.waited.get(t.dkey, 0) < t.dcnt:
                wl.append((t.dkey, t.dcnt))
        for n in ("pe", "act", "dve", "pool"):
            if self.E[n].cnt > 0:
                wl.append((n, self.E[n].cnt))
        E.ops.append((wl, None, None))

    def replay(self, block):
        sems = self.sems

        def mk(name):
            E = self.E[name]

            def body(e):
                for wl, fn, inc in E.ops:
                    for s, v in wl:
                        e.wait_ge(sems[s], v)
                    if fn is None:
                        continue
                    ins = fn(e)
                    if inc is not None:
                        ins.then_inc(sems[inc[0]], inc[1])
            return body
        block.sync(mk("sp"))
        block.gpsimd(mk("pool"))
        block.scalar(mk("act"))
        block.vector(mk("dve"))
        block.tensor(mk("pe"))


class Ring:
    def __init__(self, k, name, shape, dt, n):
        self.b = [k.sb("%s%d" % (name, i), shape, dt) for i in range(n)]
        self.i = 0

    def next(self):
        t = self.b[self.i % len(self.b)]
        self.i += 1
        return t


def build_program(SEQ=SEQ, NSTAGES=6, WITH_SAMPLE=True, PB=PB):
    NROW = PB + SB
    nc = bass.Bass("TRN2", target_bir_lowering=False)
    es = ExitStack()
    k = KB(nc, es)
    k.init_psum()

    def din(name, shape):
        return k.dram(name, shape, F32, kind="ExternalInput")

    def dout(name, shape):
        return k.dram(name, shape, F32, kind="ExternalOutput")

    xp = din("xp", [PB, SEQ, D]); xs = din("xs", [SB * SL, D])
    ck = din("ck", [DEPTH, SB, 128, 128]); cv = din("cv", [DEPTH, SB, 128, 128])
    stc = din("stc", [DEPTH, SB * 3, 512]); stl = din("stl", [DEPTH, SB, 512])
    crow = din("crow", [NROW, D])
    w_ada = din("w_ada", [DEPTH, D, 9 * D]); b_ada = din("b_ada", [DEPTH, 72, 128])
    w1g = din("w1g", [DEPTH, D, DFF]); w1u = din("w1u", [DEPTH, D, DFF]); w1d = din("w1d", [DEPTH, DFF, D])
    w_in = din("w_in", [DEPTH, D, 1792])
    q_gain = din("q_gain", [DEPTH, 64]); k_gain = din("k_gain", [DEPTH, 64]); sinks = din("sinks", [DEPTH, 8])
    conv_w = din("conv_w", [DEPTH, 16, 128]); conv_b = din("conv_b", [DEPTH, 4, 128])
    w_rg = din("w_rg", [DEPTH, 8, 64, 64]); b_rg = din("b_rg", [DEPTH, 4, 128])
    w_ig = din("w_ig", [DEPTH, 8, 64, 64]); b_ig = din("b_ig", [DEPTH, 4, 128])
    lam = din("lam", [DEPTH, 4, 128]); beta_a = din("beta_a", [DEPTH, 4, 128]); beta_l = din("beta_l", [DEPTH, 4, 128])
    w_out = din("w_out", [DEPTH, D, D])
    w2g = din("w2g", [DEPTH, D, DFF]); w2u = din("w2u", [DEPTH, D, DFF]); w2d = din("w2d", [DEPTH, DFF, D])
    c_ident = din("c_ident", [128, 128]); c_mprev = din("c_mprev", [128, 8 * 128]); c_mcur = din("c_mcur", [128, 8 * 128])

    yp = dout("yp", [PB, SEQ, D]); ys = dout("ys", [SB * SL, D])
    kpo = dout("kpo", [DEPTH, PB, 128, 128]); vpo = dout("vpo", [DEPTH, PB, 128, 128])
    cpo = dout("cpo", [DEPTH, PB, 3, 512]); hpo = dout("hpo", [DEPTH, PB, 512])
    kso = dout("kso", [DEPTH, SB, 128, 128]); vso = dout("vso", [DEPTH, SB, 128, 128])
    cso = dout("cso", [DEPTH, SB * 3, 512]); hso = dout("hso", [DEPTH, SB, 512])

    S = {}
    for l in range(DEPTH):
        S[("g", 1, l)] = k.dram("s_w1g%d" % l, [D, DFF], BF16)
        S[("u", 1, l)] = k.dram("s_w1u%d" % l, [D, DFF], BF16)
        S[("d", 1, l)] = k.dram("s_w1d%d" % l, [DFF, D], BF16)
        S[("g", 2, l)] = k.dram("s_w2g%d" % l, [D, DFF], BF16)
        S[("u", 2, l)] = k.dram("s_w2u%d" % l, [D, DFF], BF16)
        S[("d", 2, l)] = k.dram("s_w2d%d" % l, [DFF, D], BF16)
        S[("in", l)] = k.dram("s_win%d" % l, [D, WIN_COLS], BF16)
        S[("out", l)] = k.dram("s_wout%d" % l, [D, D], BF16)
    SRC = {("g", 1): w1g, ("u", 1): w1u, ("d", 1): w1d, ("g", 2): w2g, ("u", 2): w2u, ("d", 2): w2d}

    ident = k.sb("ident", [128, 128], F32)
    mprev = k.sb("mprev", [128, 8, 128], BF16)
    mcur = k.sb("mcur", [128, 8, 128], BF16)
    ones_bf = k.sb("ones_bf", [128, 128], BF16)
    blk64 = k.sb("blk64", [128, 128], BF16)
    ones_e = k.sb("ones_e", [128, 128], BF16)
    ones_o = k.sb("ones_o", [128, 128], BF16)
    consts = k.sb("constsem", [1, 4], F32)
    epsc = k.sb("epsc", [128, 1], F32)
    modT = [k.sb("modT%d" % l, [128, 72, NROW], F32) for l in range(DEPTH)]
    bT = [k.sb("bT%d" % l, [128, 72], F32) for l in range(DEPTH)]
    vecT = [k.sb("vecT%d" % l, [128, NVCOL], F32) for l in range(DEPTH)]
    wgate = [k.sb("wgate%d" % l, [128, 2, 4, 128], BF16) for l in range(DEPTH)]
    cT = k.sb("cT", [128, 8, NROW], BF16)

    xTs = [k.sb("xT%d" % i, [128, NCH, TT], F32) for i in range(2)]
    hTs = [k.sb("hT%d" % i, [128, NCH, TT], BF16) for i in range(2)]
    rstds = {"F": k.sb("rstdF", [128, TT], F32), "M": k.sb("rstdM", [128, TT], F32)}
    actT = k.sb("actT", [128, NF, TT], BF16)
    mq = k.sb("mq", [128, NCH, TT], BF16)
    knf = k.sb("knf", [128, 128], F32)
    xr = k.sb("xr", [128, 4, TT + 3], F32)
    ggr = k.sb("ggr", [128, 4, TT], F32)
    attnT = k.sb("attnT", [128, 4, TT], F32)
    tmps = {"F": Ring(k, "tmpF", [128, TT], F32, 2), "M": Ring(k, "tmpM", [128, TT], F32, 4)}
    xcbufs = Ring(k, "xcb", [128, TT], F32, 2)
    tmpb = Ring(k, "tmpb", [128, TT], BF16, 2)
    Ering = Ring(k, "E", [128, 4, 128], BF16, 6)
    vring = Ring(k, "vpad", [128, 4, 128], BF16, 6)
    WAs = {"F": Ring(k, "WAF", [128, 8, 256], BF16, 4), "M": Ring(k, "WAM", [128, 8, 256], BF16, 2)}
    WBs = {"F": Ring(k, "WBF", [128, 4, 256], BF16, 3), "M": Ring(k, "WBM", [128, 4, 256], BF16, 2)}
    kcarry = [k.sb("kcarry%d" % l, [128, 4, 128], BF16) for l in range(DEPTH)]
    vcarry = [k.sb("vcarry%d" % l, [128, 4, 128], BF16) for l in range(DEPTH)]
    ctail = [k.sb("ctail%d" % l, [128, 4, 3], F32) for l in range(DEPTH)]
    hstate = [k.sb("hstate%d" % l, [128, 4], F32) for l in range(DEPTH)]
    small = Ring(k, "small", [128, 512], F32, 4)
    ckr = Ring(k, "ckr", [128, 2, 2, 64], F32, 1)
    cvr = Ring(k, "cvr", [128, 128], F32, 1)
    kps = Ring(k, "kps", [128, 4, 128], BF16, 1)
    h0s = k.sb("h0s", [128, 4, SB], F32)
    cachecp = k.sb("cachecp", [1, 4], F32)

    def A(eng, fn, reads, writes, inc=True):
        k.op(eng, fn, reads, writes, inc)

    def act(out_ap, in_ap, func, reads, writes, bias=None, scale=None):
        kw = {}
        if bias is not None:
            kw["bias"] = bias
        if scale is not None:
            kw["scale"] = scale
        A("act", lambda e: e.activation(out=out_ap, in_=in_ap, func=func, **kw), reads, writes)

    def tt(eng, out_ap, in0, in1, op, reads, writes):
        A(eng, lambda e: e.tensor_tensor(out=out_ap, in0=in0, in1=in1, op=op), reads, writes)

    def ts(eng, out_ap, in0, s1, s2, op0, op1, reads, writes):
        if s2 is None:
            A(eng, lambda e: e.tensor_scalar(out=out_ap, in0=in0, scalar1=s1, scalar2=None, op0=op0), reads, writes)
        else:
            A(eng, lambda e: e.tensor_scalar(out=out_ap, in0=in0, scalar1=s1, scalar2=s2, op0=op0, op1=op1), reads, writes)

    def stt(eng, out_ap, in0, scalar, in1, op0, op1, reads, writes):
        A(eng, lambda e: e.scalar_tensor_tensor(out=out_ap, in0=in0, scalar=scalar, in1=in1, op0=op0, op1=op1), reads, writes)

    def cp(eng, out_ap, in_ap, reads, writes):
        if eng == "act":
            A("act", lambda e: e.copy(out=out_ap, in_=in_ap), reads, writes)
        else:
            A(eng, lambda e: e.tensor_copy(out=out_ap, in_=in_ap), reads, writes)

    def mm(out_ap, lhsT, rhs, start, stop, reads, writes, last):
        A("pe", lambda e: e.matmul(out_ap, lhsT=lhsT, rhs=rhs, start=start, stop=stop), reads, writes, inc=last)

    def tr(out_ap, in_ap, kk, reads, writes, last):
        A("pe", lambda e: e.transpose(out_ap, in_ap, ident.t[0:kk, 0:kk]), list(reads) + [ident], writes, inc=last)

    def memset(eng, ap, val, writes):
        A(eng, lambda e: e.memset(ap, val), [], writes)

    k.dma("sp", ident.t[:, :], c_ident.t[:, :], [], [ident], semt=consts, fill=True)
    k.dma("pool", mprev.t[:, :, :], c_mprev.t.rearrange("p (h q) -> p h q", h=8), [], [mprev], semt=mprev)
    k.dma("pool", mcur.t[:, :, :], c_mcur.t.rearrange("p (h q) -> p h q", h=8), [], [mcur], semt=mcur)
    memset("pool", ones_bf.t[:, :], 1.0, [ones_bf])
    memset("pool", epsc.t[:, :], EPS, [epsc])
    memset("pool", blk64.t[:, :], 0.0, [blk64])
    memset("pool", blk64.t[0:64, 0:64], 1.0, [blk64])
    memset("pool", blk64.t[64:128, 64:128], 1.0, [blk64])
    memset("pool", ones_e.t[:, :], 0.0, [ones_e])
    memset("pool", ones_e.t[:, 0:64], 1.0, [ones_e])
    memset("pool", ones_o.t[:, :], 0.0, [ones_o])
    memset("pool", ones_o.t[:, 64:128], 1.0, [ones_o])
    for t_ in vring.b + kps.b + [mq]:
        memset("pool", t_.t[:, :, :], 0.0, [t_])
    for l in range(DEPTH):
        memset("pool", vcarry[l].t[:, :, :], 0.0, [vcarry[l]])
        memset("dve", wgate[l].t[:, :, :, :], 0.0, [wgate[l]])

    vrow = []
    for l in range(DEPTH):
        v = small.next()
        vrow.append(v)

        def ld(r0, n, src_ap, l=l, v=v):
            k.dma("sp", v.t[r0:r0 + n, 0:128], src_ap, [], [v], semt=v, fill=True)
        ld(R_CW, 16, conv_w.t[l])
        ld(R_CB, 4, conv_b.t[l]); ld(R_BRG, 4, b_rg.t[l]); ld(R_BIG, 4, b_ig.t[l])
        ld(R_LAM, 4, lam.t[l]); ld(R_BA, 4, beta_a.t[l]); ld(R_BL, 4, beta_l.t[l])
        for hh in range(2):
            k.dma("sp", v.t[R_QG:R_QG + 1, hh * 64:(hh + 1) * 64], q_gain.t[l:l + 1, :], [], [v], semt=v, fill=True)
            k.dma("sp", v.t[R_KG:R_KG + 1, hh * 64:(hh + 1) * 64], k_gain.t[l:l + 1, :], [], [v], semt=v, fill=True)
        for gi_, wsrc in enumerate((w_rg, w_ig)):
            for c in range(4):
                for par in range(2):
                    k.dma("pool", wgate[l].t[par * 64:(par + 1) * 64, gi_, c, par * 64:(par + 1) * 64],
                          wsrc.t[l, 2 * c + par], [], [wgate[l]], semt=wgate[l], fill=True)

    for l in range(DEPTH):
        ps = k.psum()
        tr(ps.t[:, 0:NVROW], vrow[l].t[0:NVROW, 0:128], NVROW, [vrow[l]], [ps], True)
        cp("dve", vecT[l].t[:, 0:NVROW], ps.t[:, 0:NVROW], [ps], [vecT[l]])
        for par in range(2):
            k.dma("sp", vecT[l].t[par * 64:(par + 1) * 64, R_SK:R_SK + 4], sinks.t[l, par:8:2].partition_broadcast(64),
                  [], [vecT[l]], semt=vecT[l], nc_ok=True)
        brow = small.next()
        k.dma("sp", brow.t[0:72, 0:128], b_ada.t[l], [], [brow], semt=brow)
        ps = k.psum()
        tr(ps.t[:, 0:72], brow.t[0:72, 0:128], 72, [brow], [ps], True)
        cp("dve", bT[l].t[:, :], ps.t[:, 0:72], [ps], [bT[l]])
        t1 = tmps["M"].next()
        act(t1.t[:, 0:4], vecT[l].t[:, R_LAM:R_LAM + 4], AF.Exp, [vecT[l]], [t1], scale=-1.0)
        act(t1.t[:, 0:4], t1.t[:, 0:4], AF.Ln, [t1], [t1], bias=1.0)
        ts("dve", vecT[l].t[:, C_CL:C_CL + 4], t1.t[:, 0:4], -8.0, None, ALU.mult, None, [t1, vecT[l]], [vecT[l]])
        ts("dve", vecT[l].t[:, C_CL2:C_CL2 + 4], t1.t[:, 0:4], -16.0, None, ALU.mult, None, [t1, vecT[l]], [vecT[l]])
        act(vecT[l].t[:, C_SE:C_SE + 4], vecT[l].t[:, R_SK:R_SK + 4], AF.Exp, [vecT[l]], [vecT[l]])

    t1 = xTs[1]
    k.dma("sp", t1.t[0:NROW, 0:2, :].rearrange("p a b -> p (a b)"), crow.t[:, :], [], [t1], semt=t1)
    act(t1.t[0:NROW, 0:2, :], t1.t[0:NROW, 0:2, :], AF.Silu, [t1], [t1])
    ps = k.psum()
    for kc in range(8):
        tr(ps.t[:, kc * NROW:(kc + 1) * NROW], t1.t[0:NROW, kc // 4, (kc % 4) * 128:(kc % 4 + 1) * 128], NROW, [t1], [ps], kc == 7)
    cp("dve", cT.t[:, :, :], ps.t[:, 0:8 * NROW].rearrange("p (a b) -> p a b", a=8), [ps], [cT])

    def convert_plain(dst, src_ap, rows, cols):
        nb = 4
        rb = rows // nb
        for i in range(nb):
            k.dma("pool", dst.t[i * rb:(i + 1) * rb, :], src_ap[i * rb:(i + 1) * rb, :], [], [dst], semt=dst, fill=True)

    def convert_ffn(which, l):
        convert_plain(S[("g", which, l)], SRC[("g", which)].t[l], D, DFF)
        convert_plain(S[("u", which, l)], SRC[("u", which)].t[l], D, DFF)
        convert_plain(S[("d", which, l)], SRC[("d", which)].t[l], DFF, D)

    def convert_mixer(l):
        dst = S[("in", l)]
        src = w_in.t[l]

        def c1(d0, s0, n):
            k.dma("pool", dst.t[:, d0:d0 + n], src[:, s0:s0 + n], [], [dst], semt=dst, fill=True)
        c1(0, 0, 512)
        for g in range(2):
            for rep in range(2):
                c1(512 + g * 128 + rep * 64, 512 + g * 64, 64)
        c1(768, 768, 1024)
        c1(1792, 640, 128)
        convert_plain(S[("out", l)], w_out.t[l], D, D)

    def compute_mod(l, groups):
        for g in groups:
            ps = k.psum()
            for jj in range(8):
                wt = WAs["F"].next()
                c0 = g * D + jj * 128
                wv = wt.t.bitcast(F32)
                k.dma("sp", wv[:, :, :], w_ada.t[l][:, c0:c0 + 128].rearrange("(kc p) n -> p kc n", p=128), [], [wt], semt=wt)
                wb = WAs["F"].next()
                cp("act" if jj % 2 == 0 else "dve", wb.t[:, :, 0:128], wv[:, :, :], [wt], [wb])
                for kc in range(8):
                    mm(ps.t[:, jj * NROW:(jj + 1) * NROW], wb.t[:, kc, 0:128], cT.t[:, kc, :],
                       kc == 0, kc == 7, [wb, cT], [ps], last=(kc == 7))
            dst = modT[l].t[:, g * 8:(g + 1) * 8, :]
            tt("dve", dst, ps.t[:, 0:8 * NROW].rearrange("p (a b) -> p a b", a=8),
               bT[l].t[:, g * 8:(g + 1) * 8].unsqueeze(2).to_broadcast([128, 8, NROW]), ALU.add, [ps, bT[l], modT[l]], [modT[l]])
            if g in (1, 4, 7):
                ts("dve", dst, dst, 1.0, None, ALU.add, None, [modT[l]], [modT[l]])
            if g in (2, 8):
                ts("dve", dst, dst, 0.5, None, ALU.mult, None, [modT[l]], [modT[l]])

    class Ctx:
        pass

    def set_stream(cx, st):
        cx.st = st
        cx.tmp = tmps[st]
        cx.WA = WAs[st]
        cx.WB = WBs[st]
        cx.rstd = rstds[st]

    def ps_(cx):
        return k.psum(cx.st)

    def mod_sc(cx, l, g, dc):
        return modT[l].t[:, g * 8 + dc, cx.p:cx.p + 1]

    def mod_bc(cx, l, g, dc):
        return modT[l].t[:, g * 8 + dc, PB:PB + SB].unsqueeze(2).to_broadcast([128, SB, SL])

    def v3(ap):
        return ap.rearrange("p (b i) -> p b i", i=SL)

    def rstd_from_psum(cx, ps, T, n):
        rstd = cx.rstd
        act(rstd.t[:, 0:T], ps.t[:, 0:T], AF.Ln, [ps, epsc], [rstd], bias=epsc.t[:, 0:1], scale=1.0 / n)
        act(rstd.t[:, 0:T], rstd.t[:, 0:T], AF.Exp, [rstd], [rstd], scale=-0.5)

    def norm_mod(cx, l, gsh, gsc, sq):
        T = cx.T
        xT = cx.xT
        hT = cx.hT
        rstd = cx.rstd
        sq = hT
        ps = ps_(cx)
        for kc in range(8):
            mm(ps.t[:, 0:T], ones_bf.t[:, :], sq.t[:, kc, 0:T], kc == 0, kc == 7, [ones_bf, sq], [ps], last=(kc == 7))
        rstd_from_psum(cx, ps, T, float(D))
        for dc in range(8):
            t1 = cx.tmp.next()
            if cx.kind == "p":
                eng = "dve" if dc % 2 == 0 else "pool"
                tt(eng, t1.t[:, 0:T], xT.t[:, dc, 0:T], rstd.t[:, 0:T], ALU.mult, [xT, rstd], [t1])
                act(hT.t[:, dc, 0:T], t1.t[:, 0:T], AF.Identity, [t1, modT[l]], [hT],
                    bias=mod_sc(cx, l, gsh, dc), scale=mod_sc(cx, l, gsc, dc))
            else:
                tt("dve", t1.t[:, 0:T], xT.t[:, dc, 0:T], rstd.t[:, 0:T], ALU.mult, [xT, rstd], [t1])
                tt("dve", v3(t1.t[:, 0:T]), v3(t1.t[:, 0:T]), mod_bc(cx, l, gsc, dc), ALU.mult, [t1, modT[l]], [t1])
                tt("pool", v3(hT.t[:, dc, 0:T]), v3(t1.t[:, 0:T]), mod_bc(cx, l, gsh, dc), ALU.add, [t1, modT[l]], [hT])
        yield 7.0

    def resid_update(cx, l, gg, dc, py):
        T = cx.T
        xT = cx.xT
        if cx.kind == "p":
            stt("dve", xT.t[:, dc, 0:T], py.t[:, 0:T], mod_sc(cx, l, gg, dc), xT.t[:, dc, 0:T], ALU.mult, ALU.add,
                [py, modT[l], xT], [xT])
        else:
            t1 = cx.tmp.next()
            tt("dve", v3(t1.t[:, 0:T]), v3(py.t[:, 0:T]), mod_bc(cx, l, gg, dc), ALU.mult, [py, modT[l]], [t1])
            tt("pool", xT.t[:, dc, 0:T], xT.t[:, dc, 0:T], t1.t[:, 0:T], ALU.add, [xT, t1], [xT])

    def proj_out(cx, l, gg, src_tile, nk, wsrc):
        T = cx.T
        q = cx.st
        dq_ = "sp" if q == "F" else MQ
        for qp in range(4):
            pys = [ps_(cx) for _ in range(2)]
            for k0 in range(0, nk, 4):
                nj = min(4, nk - k0)
                wt = cx.WB.next()
                k.dma(dq_, wt.t[:, 0:nj, :],
                      wsrc.t[k0 * 128:(k0 + nj) * 128, qp * 256:(qp + 1) * 256].rearrange("(j p) d -> p j d", p=128),
                      [wsrc], [wt], semt=wt)
                for j in range(nj):
                    kk = k0 + j
                    for dq in range(2):
                        mm(pys[dq].t[:, 0:T], wt.t[:, j, dq * 128:(dq + 1) * 128], src_tile.t[:, kk, 0:T],
                           kk == 0, kk == nk - 1, [wt, src_tile], [pys[dq]], last=(kk == nk - 1 or (j == nj - 1 and dq == 1)))
                yield 0.26 * nj * 2 * T / 512 + 0.3
            for dq in range(2):
                dc = qp * 2 + dq
                resid_update(cx, l, gg, dc, pys[dq])
                if cx.sqnext:
                    act(cx.hT.t[:, dc, 0:T], cx.xT.t[:, dc, 0:T], AF.Square, [cx.xT], [cx.hT])
            yield 0.5

    def ffn(cx, l, which):
        set_stream(cx, "F")
        T = cx.T
        hT = cx.hT
        g0 = 0 if which == 1 else 6
        sg = S[("g", which, l)]
        su = S[("u", which, l)]
        for u in range(NF // 2):
            wg = cx.WA.next()
            k.dma("sp", wg.t[:, :, :], sg.t[:, u * 256:(u + 1) * 256].rearrange("(kc p) f -> p kc f", p=128), [sg], [wg], semt=wg)
            wu = cx.WA.next()
            k.dma("sp", wu.t[:, :, :], su.t[:, u * 256:(u + 1) * 256].rearrange("(kc p) f -> p kc f", p=128), [su], [wu], semt=wu)
            for j in range(2):
                f = u * 2 + j
                pg = ps_(cx)
                for kc in range(8):
                    mm(pg.t[:, 0:T], wg.t[:, kc, j * 128:(j + 1) * 128], hT.t[:, kc, 0:T], kc == 0, kc == 7, [wg, hT], [pg], last=(kc == 7))
                pu = ps_(cx)
                for kc in range(8):
                    mm(pu.t[:, 0:T], wu.t[:, kc, j * 128:(j + 1) * 128], hT.t[:, kc, 0:T], kc == 0, kc == 7, [wu, hT], [pu], last=(kc == 7))
                t1 = cx.tmp.next()
                act(t1.t[:, 0:T], pg.t[:, 0:T], AF.Silu, [pg], [t1])
                tt("dve", actT.t[:, f, 0:T], t1.t[:, 0:T], pu.t[:, 0:T], ALU.mult, [t1, pu], [actT])
                yield 4.2 * T / 512 + 0.6
        yield from proj_out(cx, l, g0 + 2, actT, NF, S[("d", which, l)])

    def attn_block(cx, l, tok0, nq, kprev, vprev_t, vcur_t):
        has_prev = kprev is not None
        vl = vecT[l]
        Es = {}
        for g in range(2):
            srcs = []
            if has_prev:
                srcs.append(("p", 128))
            srcs.append(("c", nq))
            for kind, nk in srcs:
                ps = ps_(cx)
                for hh in range(4):
                    h = 4 * g + hh
                    if kind == "p":
                        kt, kch, kc0 = kprev
                        lhsT = kt.t[:, kch + 2 * g + (h % 2), kc0:kc0 + 128]
                        rd = [kt, mq]
                    else:
                        lhsT = mq.t[:, 4 + 2 * g + (h % 2), tok0:tok0 + nq]
                        rd = [mq]
                    mm(ps.t[0:nk, hh * 128:hh * 128 + nq], lhsT, mq.t[:, h // 2, tok0:tok0 + nq], True, True, rd, [ps], last=(hh == 3))
                psv = ps.t[0:nk, :].rearrange("p (h q) -> p h q", h=4)[:, :, 0:nq]
                t1 = cx.tmp.next()
                t1v = t1.t[0:nk, :].rearrange("p (h q) -> p h q", h=4)[:, :, 0:nq]
                act(t1v, psv, AF.Exp, [ps], [t1], scale=0.125)
                Et = Ering.next()
                mt = mprev if kind == "p" else mcur
                tt("pool" if (kind == "p" or nq != 128) else "dve", Et.t[0:nk, :, 0:nq], t1v, mt.t[0:nk, 4 * g:4 * g + 4, 0:nq], ALU.mult, [t1, mt], [Et])
                Es[(g, kind)] = (Et, nk)
        yield (7.0 if nq == 128 else 1.5) * MSC
        po = ps_(cx)
        pd = ps_(cx)
        for which_, pt in (("o", po), ("d", pd)):
            for c in range(4):
                g = c // 2
                lst = []
                for par in range(2):
                    hh = (2 * c + par) % 4
                    for kind in (("p", "c") if has_prev else ("c",)):
                        Et, nk = Es[(g, kind)]
                        if which_ == "o":
                            vt = vprev_t if kind == "p" else vcur_t
                            lhsT = vt.t[0:nk, 2 * g + par, :]
                            rd = [vt, Et]
                        else:
                            ot = ones_e if par == 0 else ones_o
                            lhsT = ot.t[0:nk, :]
                            rd = [ot, Et]
                        lst.append((lhsT, Et.t[0:nk, hh, 0:nq], rd))
                for i, (lhsT, rhs, rd) in enumerate(lst):
                    mm(pt.t[:, c * 128:c * 128 + nq], lhsT, rhs, i == 0, i == len(lst) - 1, rd, [pt], last=(i == len(lst) - 1))
        pov = po.t[:, :].rearrange("p (c q) -> p c q", c=4)[:, :, 0:nq]
        pdv = pd.t[:, :].rearrange("p (c q) -> p c q", c=4)[:, :, 0:nq]
        t2 = cx.tmp.next()
        t2v = t2.t[:, :].rearrange("p (c q) -> p c q", c=4)[:, :, 0:nq]
        tt("dve", t2v, pdv, vl.t[:, C_SE:C_SE + 4].unsqueeze(2).to_broadcast([128, 4, nq]), ALU.add, [pd, vl], [t2])
        A("dve", lambda e: e.reciprocal(out=t2v, in_=t2v), [t2], [t2])
        tt("dve", attnT.t[:, :, tok0:tok0 + nq], pov, t2v, ALU.mult, [po, t2], [attnT])
        yield 4.0 if nq == 128 else 1.5

    def mixer(cx, l):
        set_stream(cx, "M")
        T = cx.T
        hT = cx.hT
        vl = vecT[l]
        memset("pool", mq.t[64:128, 4:8:2, :], 0.0, [mq])
        memset("pool", mq.t[0:64, 5:8:2, :], 0.0, [mq])
        sw = S[("in", l)]

        def load_win(u, ncols=256):
            wt = cx.WA.next()
            k.dma(MQ, wt.t[:, :, 0:ncols], sw.t[:, u * 256:u * 256 + ncols].rearrange("(kc p) f -> p kc f", p=128), [sw], [wt], semt=wt)
            return wt

        def zchunk(wt, j):
            ps = ps_(cx)
            for kc in range(8):
                mm(ps.t[:, 0:T], wt.t[:, kc, j * 128:(j + 1) * 128], hT.t[:, kc, 0:T], kc == 0, kc == 7, [wt, hT], [ps], last=(kc == 7))
            return ps

        def headnorm_a(ps):
            sq = tmpb.next()
            act(sq.t[:, 0:T], ps.t[:, 0:T], AF.Square, [ps], [sq])
            return sq

        def headnorm_b(ps, sq):
            p2 = ps_(cx)
            mm(p2.t[:, 0:T], blk64.t[:, :], sq.t[:, 0:T], True, True, [blk64, sq], [p2], last=True)
            r = cx.tmp.next()
            act(r.t[:, 0:T], p2.t[:, 0:T], AF.Ln, [p2, epsc], [r], bias=epsc.t[:, 0:1], scale=1.0 / 64)
            act(r.t[:, 0:T], r.t[:, 0:T], AF.Exp, [r], [r], scale=-0.5)
            return r

        need_kout = (cx.kind == "s") or cx.last
        for u in range(3):
            wt = load_win(u)
            for j in range(2):
                c = u * 2 + j
                ps = zchunk(wt, j)
                sq = headnorm_a(ps)
                yield (4.0 * T / 512 + 0.5) * MSC
                r = headnorm_b(ps, sq)
                if c < 4:
                    stt("dve", mq.t[:, c, 0:T], ps.t[:, 0:T], vl.t[:, R_QG:R_QG + 1], r.t[:, 0:T], ALU.mult, ALU.mult, [ps, vl, r], [mq])
                else:
                    g = c - 4
                    for par in range(2):
                        lo, hi = par * 64, (par + 1) * 64
                        stt("dve", mq.t[lo:hi, 4 + 2 * g + par, 0:T], ps.t[lo:hi, 0:T], vl.t[lo:hi, R_KG:R_KG + 1], r.t[lo:hi, 0:T],
                            ALU.mult, ALU.mult, [ps, vl, r], [mq])
                    if need_kout:
                        lo, hi = g * 64, (g + 1) * 64
                        stt("dve", knf.t[lo:hi, :], ps.t[lo:hi, T - 128:T], vl.t[lo:hi, R_KG:R_KG + 1], r.t[lo:hi, T - 128:T],
                            ALU.mult, ALU.mult, [ps, vl, r], [knf])
                yield 1.0
        if cx.kind == "p":
            xrv = lambda c, off, n: xr.t[:, c, off:off + n]
            cp("pool", xr.t[:, :, 0:3], ctail[l].t[:, :, :], [ctail[l]], [xr])
        else:
            xr4 = lambda c: xr.t[:, c, 0:SB * 11].rearrange("p (b i) -> p b i", i=11)
            xrv = lambda c, off, n: xr4(c)[:, :, off:off + n]
            stc_sb = small.next()
            k.dma("sp", stc_sb.t[0:SB * 3, :], stc.t[l], [], [stc_sb], semt=stc_sb)
            ps = ps_(cx)
            for c in range(4):
                tr(ps.t[:, c * 48:(c + 1) * 48], stc_sb.t[0:SB * 3, c * 128:(c + 1) * 128], 48, [stc_sb], [ps], c == 3)
            for c in range(4):
                cp("dve", xrv(c, 0, 3), ps.t[:, c * 48:(c + 1) * 48].rearrange("p (b i) -> p b i", i=3), [ps], [xr])
        for u in range(2):
            wt = load_win(3 + u)
            for j in range(2):
                c = u * 2 + j
                ps = zchunk(wt, j)
                if cx.kind == "p":
                    cp("act", xrv(c, 3, T), ps.t[:, 0:T], [ps], [xr])
                else:
                    cp("act", xrv(c, 3, SL), v3(ps.t[:, 0:T]), [ps], [xr])
                yield 2.2 * T / 512 + 0.3
        for u in range(2):
            wt = load_win(5 + u)
            for j in range(2):
                c = u * 2 + j
                ps = zchunk(wt, j)
                act(ggr.t[:, c, 0:T], ps.t[:, 0:T], AF.Gelu_apprx_tanh, [ps], [ggr])
                yield 2.2 * T / 512 + 0.3
        wt = load_win(7, 128)
        nblk = T // 128 if cx.kind == "p" else SB
        nq = 128 if cx.kind == "p" else SL

        def vgroup(b0):
            vts = []
            ps = ps_(cx)
            for bb in range(4):
                blk = b0 + bb
                for kc in range(8):
                    mm(ps.t[0:nq, bb * 128:(bb + 1) * 128], hT.t[:, kc, blk * nq:(blk + 1) * nq], wt.t[:, kc, 0:128],
                       kc == 0, kc == 7, [hT, wt], [ps], last=(kc == 7))
            for bb in range(4):
                blk = b0 + bb
                vt = vring.next()
                src = ps.t[0:nq, bb * 128:(bb + 1) * 128].rearrange("p (g d) -> p g d", g=2)
                cp("act", vt.t[0:nq, 0:4:2, 0:64], src, [ps], [vt])
                cp("act", vt.t[0:nq, 1:4:2, 64:128], src, [ps], [vt])
                vts.append(vt)
                if cx.kind == "p" and cx.last and blk == nblk - 1:
                    st = small.next()
                    cp("act", st.t[:, 0:128], ps.t[:, bb * 128:(bb + 1) * 128], [ps], [st])
                    k.dma("pool", vpo.t[l, cx.p], st.t[:, 0:128], [st], [], semt=st)
            if cx.kind == "s":
                st = small.next()
                cp("act", st.t[0:SL, :], ps.t[0:SL, :], [ps], [st])
                k.dma("pool", vso.t[l, b0:b0 + 4, 120:128, :].rearrange("b i f -> i b f"),
                      st.t[0:SL, :].rearrange("p (b f) -> p b f", b=4), [st], [], semt=st)
            return vts

        def lru_a(c):
            xc = xcbufs.next()
            if cx.kind == "p":
                xcv = xc.t[:, 0:T]
                L = T
            else:
                xcv = v3(xc.t[:, 0:T])
                L = SL
            ts("dve", xcv, xrv(c, 0, L), vl.t[:, R_CW + c:R_CW + c + 1], vl.t[:, R_CB + c:R_CB + c + 1], ALU.mult, ALU.add, [xr, vl], [xc])
            for tap in range(1, 4):
                stt("dve", xcv, xrv(c, tap, L), vl.t[:, R_CW + tap * 4 + c:R_CW + tap * 4 + c + 1], xcv, ALU.mult, ALU.add, [xr, vl, xc], [xc])
            xcb = tmpb.next()
            cp("pool", xcb.t[:, 0:T], xc.t[:, 0:T], [xc], [xcb])
            return xc, xcb

        def lru_b(c, xc, xcb):
            st_ = lru_b1(c, xc, xcb)
            lru_b2(c, xc, st_)

        def lru_b1(c, xc, xcb):
            pr = ps_(cx)
            mm(pr.t[:, 0:T], wgate[l].t[:, 0, c, :], xcb.t[:, 0:T], True, True, [wgate[l], xcb], [pr], last=True)
            pi_ = ps_(cx)
            mm(pi_.t[:, 0:T], wgate[l].t[:, 1, c, :], xcb.t[:, 0:T], True, True, [wgate[l], xcb], [pi_], last=True)
            ra = cx.tmp.next()
            act(ra.t[:, 0:T], pr.t[:, 0:T], AF.Sigmoid, [pr, vl], [ra], bias=vl.t[:, R_BRG + c:R_BRG + c + 1])
            gi = cx.tmp.next()
            act(gi.t[:, 0:T], pi_.t[:, 0:T], AF.Sigmoid, [pi_, vl], [gi], bias=vl.t[:, R_BIG + c:R_BIG + c + 1])
            a2 = cx.tmp.next()
            act(a2.t[:, 0:T], ra.t[:, 0:T], AF.Exp, [ra, vl], [a2], scale=vl.t[:, C_CL2 + c:C_CL2 + c + 1])
            act(ra.t[:, 0:T], ra.t[:, 0:T], AF.Exp, [ra, vl], [ra], scale=vl.t[:, C_CL + c:C_CL + c + 1])
            act(a2.t[:, 0:T], a2.t[:, 0:T], AF.Sqrt, [a2], [a2], bias=1.0, scale=-1.0)
            return (ra, gi, a2)

        def lru_b2(c, xc, st_):
            ra, gi, a2 = st_
            tt("dve", gi.t[:, 0:T], gi.t[:, 0:T], xc.t[:, 0:T], ALU.mult, [gi, xc], [gi])
            tt("dve", gi.t[:, 0:T], gi.t[:, 0:T], a2.t[:, 0:T], ALU.mult, [gi, a2], [gi])
            hs = cx.tmp.next()
            if cx.kind == "p":
                A("dve", lambda e, hs=hs, ra=ra, gi=gi, c=c: e.tensor_tensor_scan(
                    out=hs.t[:, 0:T], data0=ra.t[:, 0:T], data1=gi.t[:, 0:T], initial=hstate[l].t[:, c:c + 1],
                    op0=ALU.mult, op1=ALU.add), [ra, gi, hstate[l]], [hs])
                cp("act", hstate[l].t[:, c:c + 1], hs.t[:, T - 1:T], [hs], [hstate[l]])
            else:
                a3 = v3(ra.t[:, 0:T])
                u3 = v3(gi.t[:, 0:T])
                t16 = small.next()
                tt("dve", t16.t[:, 0:SB], a3[:, :, 0], h0s.t[:, c, :], ALU.mult, [ra, h0s], [t16])
                tt("dve", u3[:, :, 0], u3[:, :, 0], t16.t[:, 0:SB], ALU.add, [gi, t16], [gi])
                memset("dve", a3[:, :, 0], 0.0, [ra])
                A("dve", lambda e, hs=hs, ra=ra, gi=gi: e.tensor_tensor_scan(
                    out=hs.t[:, 0:T], data0=ra.t[:, 0:T], data1=gi.t[:, 0:T], initial=0.0,
                    op0=ALU.mult, op1=ALU.add), [ra, gi], [hs])
                cp("act", h0s.t[:, c, :], v3(hs.t[:, 0:T])[:, :, SL - 1], [hs], [h0s])
            tt("pool", ggr.t[:, c, 0:T], hs.t[:, 0:T], ggr.t[:, c, 0:T], ALU.mult, [hs, ggr], [ggr])

        if cx.kind == "p":
            vtiles = vgroup(0)
            yield 2.0
            nxt = lru_a(0)
            for blk in range(nblk):
                xc, xcb = nxt
                st_ = lru_b1(blk, xc, xcb)
                if blk + 1 < nblk:
                    nxt = lru_a(blk + 1)
                yield 4.0 * MSC
                lru_b2(blk, xc, st_)
                yield 3.0 * MSC
                if blk == 0:
                    kp = None if cx.first else (kcarry[l], 0, 0)
                    vp = None if cx.first else vcarry[l]
                    yield from attn_block(cx, l, 0, 128, kp, vp, vtiles[0])
                else:
                    yield from attn_block(cx, l, blk * 128, 128, (mq, 4, (blk - 1) * 128), vtiles[blk - 1], vtiles[blk])
            if not cx.last:
                cp("pool", kcarry[l].t[:, :, :], mq.t[:, 4:8, T - 128:T], [mq], [kcarry[l]])
                cp("pool", vcarry[l].t[:, :, :], vtiles[nblk - 1].t[:, :, :], [vtiles[nblk - 1]], [vcarry[l]])
        else:
            for b0 in range(0, SB, 4):
                vts = vgroup(b0)
                for bb in range(4):
                    b = b0 + bb
                    ckt = ckr.next()
                    for rp in range(2):
                        k.dma("sp", ckt.t[:, :, rp, :], ck.t[l, b].rearrange("w (g d) -> w g d", g=2), [], [ckt], semt=ckt, fill=(rp == 1))
                    cvt = cvr.next()
                    k.dma("sp", cvt.t[:, :], cv.t[l, b], [], [cvt], semt=cvt)
                    ps = ps_(cx)
                    for g in range(2):
                        src = ckt.t[:, g, :, :].rearrange("p r d -> p (r d)")
                        tr(ps.t[:, g * 128:(g + 1) * 128], src, 128, [ckt], [ps], g == 1)
                    kpt = kps.next()
                    for par in range(2):
                        lo, hi = par * 64, (par + 1) * 64
                        cp("act", kpt.t[lo:hi, par:4:2, :], ps.t[lo:hi, 0:256].rearrange("p (g q) -> p g q", g=2), [ps], [kpt])
                    vpt = vring.next()
                    srcv = cvt.t[:, :].rearrange("p (g d) -> p g d", g=2)
                    cp("pool", vpt.t[:, 0:4:2, 0:64], srcv, [cvt], [vpt])
                    cp("pool", vpt.t[:, 1:4:2, 64:128], srcv, [cvt], [vpt])
                    yield from attn_block(cx, l, b * SL, SL, (kpt, 0, 0), vpt, vts[bb])
            for c in range(4):
                xc, xcb = lru_a(c)
                yield 2.0
                lru_b(c, xc, xcb)
                yield 5.0

        if need_kout:
            ps = ps_(cx)
            tr(ps.t[:, 0:128], knf.t[:, :], 128, [knf], [ps], True)
            st = small.next()
            cp("act", st.t[:, 0:128], ps.t[:, 0:128], [ps], [st])
            if cx.kind == "p":
                k.dma("pool", kpo.t[l, cx.p], st.t[:, 0:128], [st], [], semt=st)
            else:
                for b in range(SB):
                    k.dma("pool", kso.t[l, b, 120:128, :], st.t[b * SL:(b + 1) * SL, 0:128], [st], [], semt=st)
        if cx.kind == "p":
            cp("pool", ctail[l].t[:, :, :], xr.t[:, :, T:T + 3], [xr], [ctail[l]])
            if cx.last:
                ps = ps_(cx)
                for c in range(4):
                    tr(ps.t[0:3, c * 128:(c + 1) * 128], xr.t[:, c, T:T + 3], 128, [xr], [ps], c == 3)
                st = small.next()
                cp("act", st.t[0:3, :], ps.t[0:3, :], [ps], [st])
                k.dma("pool", cpo.t[l, cx.p], st.t[0:3, :], [st], [], semt=st)
                ps = ps_(cx)
                for c in range(4):
                    tr(ps.t[0:1, c * 128:(c + 1) * 128], hstate[l].t[:, c:c + 1], 128, [hstate[l]], [ps], c == 3)
                st = small.next()
                cp("act", st.t[0:1, :], ps.t[0:1, :], [ps], [st])
                k.dma("pool", hpo.t[l, cx.p:cx.p + 1, :], st.t[0:1, :], [st], [], semt=st)
        else:
            t48 = cx.tmp.next()
            for c in range(4):
                cp("dve", t48.t[:, c * 48:(c + 1) * 48].rearrange("p (b i) -> p b i", i=3), xrv(c, SL, 3), [xr], [t48])
            ps = ps_(cx)
            for c in range(4):
                tr(ps.t[0:48, c * 128:(c + 1) * 128], t48.t[:, c * 48:(c + 1) * 48], 128, [t48], [ps], c == 3)
            st = small.next()
            cp("act", st.t[0:48, :], ps.t[0:48, :], [ps], [st])
            k.dma("pool", cso.t[l], st.t[0:48, :], [st], [], semt=st)
            ps = ps_(cx)
            for c in range(4):
                tr(ps.t[0:SB, c * 128:(c + 1) * 128], h0s.t[:, c, :], 128, [h0s], [ps], c == 3)
            st = small.next()
            cp("act", st.t[0:SB, :], ps.t[0:SB, :], [ps], [st])
            k.dma("pool", hso.t[l], st.t[0:SB, :], [st], [], semt=st)
        yield 1.0

        rstd = cx.rstd
        for (srct, beta0, m0) in ((attnT, R_BA, 0), (ggr, R_BL, 4)):
            act(hT.t[:, 0:4, 0:T], srct.t[:, :, 0:T], AF.Square, [srct], [hT])
            yield 2.5
            ps = ps_(cx)
            for c in range(4):
                mm(ps.t[:, 0:T], ones_bf.t[:, :], hT.t[:, c, 0:T], c == 0, c == 3, [ones_bf, hT], [ps], last=(c == 3))
            rstd_from_psum(cx, ps, T, 512.0)
            for c in range(4):
                stt("dve", mq.t[:, m0 + c, 0:T], srct.t[:, c, 0:T], vl.t[:, beta0 + c:beta0 + c + 1], rstd.t[:, 0:T],
                    ALU.mult, ALU.mult, [srct, vl, rstd], [mq])
            yield 4.0
        yield from proj_out(cx, l, 5, mq, 8, S[("out", l)])

    def load_x(cx):
        set_stream(cx, "F")
        T = cx.T
        xT = cx.xT
        for blk in range(T // 128):
            for half in range(2):
                xin = small.next()
                if cx.kind == "p":
                    src = xp.t[cx.p, cx.tok0 + blk * 128:cx.tok0 + (blk + 1) * 128, half * 512:(half + 1) * 512]
                else:
                    src = xs.t[:, half * 512:(half + 1) * 512]
                k.dma("sp", xin.t[:, :], src, [], [xin], semt=xin)
                ps = ps_(cx)
                for q in range(4):
                    tr(ps.t[:, q * 128:(q + 1) * 128], xin.t[:, q * 128:(q + 1) * 128], 128, [xin], [ps], q == 3)
                cp("act" if half == 0 else "dve", xT.t[:, half * 4:half * 4 + 4, blk * 128:(blk + 1) * 128],
                   ps.t[:, :].rearrange("p (c q) -> p c q", c=4), [ps], [xT])
            yield 1.5
        act(cx.hT.t[:, :, 0:T], xT.t[:, :, 0:T], AF.Square, [xT], [cx.hT])

    def store_y(cx):
        set_stream(cx, "F")
        T = cx.T
        xT = cx.xT
        for blk in range(T // 128):
            for half in range(2):
                yo = small.next()
                ps = ps_(cx)
                for q in range(4):
                    dc = half * 4 + q
                    tr(ps.t[:, q * 128:(q + 1) * 128], xT.t[:, dc, blk * 128:(blk + 1) * 128], 128, [xT], [ps], q == 3)
                cp("act" if half == 0 else "dve", yo.t[:, :], ps.t[:, :], [ps], [yo])
                if cx.kind == "p":
                    dst = yp.t[cx.p, cx.tok0 + blk * 128:cx.tok0 + (blk + 1) * 128, half * 512:(half + 1) * 512]
                else:
                    dst = ys.t[:, half * 512:(half + 1) * 512]
                k.dma("pool", dst, yo.t[:, :], [yo], [], semt=yo)
            yield 1.5

    stage_list = []
    for l in range(DEPTH):
        stage_list += [("ffn", l, 1), ("mix", l, 0), ("ffn", l, 2)]
    stage_list = stage_list[:NSTAGES]
    NS = len(stage_list)

    def prefetch_for(si):
        if si >= NS:
            return
        kind, l, which = stage_list[si]
        if kind == "ffn":
            convert_ffn(which, l)
            compute_mod(l, (0, 1, 2) if which == 1 else (6, 7, 8))
        else:
            convert_mixer(l)
            compute_mod(l, (3, 4, 5))

    tiles = []
    if WITH_SAMPLE:
        cx = Ctx()
        cx.kind = "s"; cx.p = None; cx.T = SB * SL; cx.tok0 = 0; cx.first = True; cx.last = True
        tiles.append(cx)
    for p in range(PB):
        for tb in range(SEQ // TT):
            cx = Ctx()
            cx.kind = "p"; cx.p = p; cx.T = TT; cx.tok0 = tb * TT
            cx.first = (tb == 0); cx.last = (tb == SEQ // TT - 1)
            tiles.append(cx)
    for i, cx in enumerate(tiles):
        cx.idx = i
        cx.xT = xTs[i % 2]
        cx.hT = hTs[i % 2]

    for l in range(DEPTH):
        k.dma("pool", kso.t[l, :, 0:120, :], ck.t[l, :, 8:128, :], [], [], semt=cachecp, fill=True)
        k.dma("pool", vso.t[l, :, 0:120, :], cv.t[l, :, 8:128, :], [], [], semt=cachecp, fill=True)

    def norm_for(cx, si):
        if si >= NS:
            return
        kind, l, which = stage_list[si]
        g0 = 3 if kind == "mix" else (0 if which == 1 else 6)
        yield from norm_mod(cx, l, g0, g0 + 1, None)

    def tile_stage_gen(cx, si):
        for v_ in tile_stage_body(cx, si):
            yield v_
        if si < NS:
            yield from norm_for(cx, si + 1)

    def tile_stage_body(cx, si):
        if si == -1:
            if cx.kind == "p" and cx.first:
                for l in range(DEPTH):
                    memset("pool", ctail[l].t[:, :, :], 0.0, [ctail[l]])
                    memset("pool", hstate[l].t[:, :], 0.0, [hstate[l]])
            yield from load_x(cx)
        elif si == NS:
            yield from store_y(cx)
        else:
            kind, l, which = stage_list[si]
            cx.sqnext = (si + 1 < NS)
            if kind == "ffn":
                yield from ffn(cx, l, which)
            else:
                if cx.kind == "s":
                    set_stream(cx, "M")
                    stl_sb = small.next()
                    k.dma("sp", stl_sb.t[0:SB, :], stl.t[l], [], [stl_sb], semt=stl_sb)
                    ps = ps_(cx)
                    for c in range(4):
                        tr(ps.t[:, c * SB:(c + 1) * SB], stl_sb.t[0:SB, c * 128:(c + 1) * 128], SB, [stl_sb], [ps], c == 3)
                    cp("dve", h0s.t[:, :, :], ps.t[:, 0:4 * SB].rearrange("p (c b) -> p c b", c=4), [ps], [h0s])
                yield from mixer(cx, l)

    def drain(g):
        for _ in g:
            pass

    def run_pair(ga, gb):
        ta = tb = 0.0
        da = db = False
        while not (da and db):
            if db or (not da and ta <= tb):
                try:
                    ta += next(ga)
                except StopIteration:
                    da = True
            else:
                try:
                    tb += next(gb)
                except StopIteration:
                    db = True

    def stage_kind(si):
        if si < 0 or si >= NS:
            return "io"
        return stage_list[si][0]

    if NSTAGES > 0:
        compute_mod(0, (0, 1, 2))
        convert_ffn(1, 0)
    ti = 0
    while ti < len(tiles):
        A_ = tiles[ti]
        B_ = tiles[ti + 1] if (PIPE and A_.kind == "p" and ti + 1 < len(tiles) and tiles[ti + 1].kind == "p") else None
        seqA = list(range(-1, NS + 1))
        if B_ is None:
            for si in seqA:
                if ti == 0 and 0 <= si < NS:
                    prefetch_for(si + 1)
                drain(tile_stage_gen(A_, si))
            ti += 1
            continue
        for s in range(-1, NS + 2):
            sa = s if s <= NS else None
            sb = s - 1 if (s - 1 >= -1 and s - 1 <= NS) else None
            if ti == 0 and sa is not None and 0 <= sa < NS:
                prefetch_for(sa + 1)
            ga = tile_stage_gen(A_, sa) if sa is not None else None
            gb = tile_stage_gen(B_, sb) if sb is not None else None
            if ga is not None and gb is not None:
                ka, kb = stage_kind(sa), stage_kind(sb)
                if {ka, kb} == {"ffn", "mix"}:
                    run_pair(ga, gb)
                else:
                    drain(ga)
                    drain(gb)
            elif ga is not None:
                drain(ga)
            elif gb is not None:
                drain(gb)
        ti += 2

    k.finish()
    block = es.enter_context(nc.Block())
    k.replay(block)
    es.close()
    return nc


_CACHE = {}


def _consts():
    slopes = np.array([2.0 ** (-8.0 * (h + 1) / 8) for h in range(8)], dtype=np.float64)
    j = np.arange(128)[:, None, None]
    i = np.arange(128)[None, None, :]
    s = slopes[None, :, None]
    dprev = 128 + i - j
    mprev = np.where(j >= i, np.exp(-s * dprev), 0.0).astype(np.float32)
    dcur = i - j
    mcur = np.where(j <= i, np.exp(-s * dcur), 0.0).astype(np.float32)
    return (np.eye(128, dtype=np.float32), np.ascontiguousarray(mprev.reshape(128, 1024)),
            np.ascontiguousarray(mcur.reshape(128, 1024)))


def kernel(x_prompt, x_sample, cache_k, cache_v, state_conv, state_lru, c_prompt, c_sample,
           w_ada, b_ada, w1_gate, w1_up, w1_down, w_in, q_gain, k_gain, sinks,
           conv_w, conv_b, w_rg, b_rg, w_ig, b_ig, lru_lambda, beta_attn, beta_lru,
           w_out, w2_gate, w2_up, w2_down):
    f = lambda a: np.ascontiguousarray(np.asarray(a, dtype=np.float32))
    if "nc" not in _CACHE:
        _CACHE["nc"] = build_program()
    nc = _CACHE["nc"]
    ident, mprev, mcur = _consts()
    shared = {
        "w_ada": f(w_ada), "b_ada": f(b_ada).reshape(DEPTH, 72, 128),
        "w1g": f(w1_gate), "w1u": f(w1_up), "w1d": f(w1_down), "w_in": f(w_in),
        "q_gain": f(q_gain), "k_gain": f(k_gain), "sinks": f(sinks),
        "conv_w": f(conv_w).reshape(DEPTH, 16, 128), "conv_b": f(conv_b).reshape(DEPTH, 4, 128),
        "w_rg": f(w_rg), "b_rg": f(b_rg).reshape(DEPTH, 4, 128), "w_ig": f(w_ig), "b_ig": f(b_ig).reshape(DEPTH, 4, 128),
        "lam": f(lru_lambda).reshape(DEPTH, 4, 128), "beta_a": f(beta_attn).reshape(DEPTH, 4, 128),
        "beta_l": f(beta_lru).reshape(DEPTH, 4, 128), "w_out": f(w_out),
        "w2g": f(w2_gate), "w2u": f(w2_up), "w2d": f(w2_down),
        "c_ident": ident, "c_mprev": mprev, "c_mcur": mcur,
    }
    x_prompt = f(x_prompt); x_sample = f(x_sample); cache_k = f(cache_k); cache_v = f(cache_v)
    state_conv = f(state_conv); state_lru = f(state_lru); c_prompt = f(c_prompt); c_sample = f(c_sample)
    in_maps = []
    for c in range(NCORES):
        m = dict(shared)
        m["xp"] = x_prompt[PB * c:PB * (c + 1)]
        m["xs"] = x_sample[SB * c:SB * (c + 1)].reshape(SB * SL, D)
        m["ck"] = np.ascontiguousarray(cache_k[:, SB * c:SB * (c + 1)].reshape(DEPTH, SB, 128, 128))
        m["cv"] = np.ascontiguousarray(cache_v[:, SB * c:SB * (c + 1)].reshape(DEPTH, SB, 128, 128))
        m["stc"] = np.ascontiguousarray(state_conv[:, SB * c:SB * (c + 1)].reshape(DEPTH, SB * 3, 512))
        m["stl"] = np.ascontiguousarray(state_lru[:, SB * c:SB * (c + 1)])
        m["crow"] = np.ascontiguousarray(np.concatenate([c_prompt[PB * c:PB * (c + 1)], c_sample[SB * c:SB * (c + 1)]], axis=0))
        in_maps.append(m)
    res = run_bass_kernel_spmd(nc, in_maps, core_ids=list(range(NCORES)))
    R = res.results
    cat = lambda name, ax: np.concatenate([np.asarray(r[name]) for r in R], axis=ax)
    y_prompt = cat("yp", 0)
    y_sample = cat("ys", 0).reshape(NCORES * SB, SL, D)
    k_prompt = cat("kpo", 1).reshape(DEPTH, NCORES * PB, 128, 2, 64)
    v_prompt = cat("vpo", 1).reshape(DEPTH, NCORES * PB, 128, 2, 64)
    conv_prompt = cat("cpo", 1)
    lru_prompt = cat("hpo", 1)
    k_sample = cat("kso", 1).reshape(DEPTH, NCORES * SB, 128, 2, 64)
    v_sample = cat("vso", 1).reshape(DEPTH, NCORES * SB, 128, 2, 64)
    conv_sample = np.concatenate([np.asarray(r["cso"]).reshape(DEPTH, SB, 3, 512) for r in R], axis=1)
    lru_sample = cat("hso", 1)
    return (y_prompt, y_sample, k_prompt, v_prompt, conv_prompt, lru_prompt,
            k_sample, v_sample, conv_sample, lru_sample)
```
